# Optimizing a Trainium2 kernel written in Bass

```python
import jax, jax.numpy as jnp
from jax import lax
import numpy as np

D_MODEL = 1024
BATCH = 8
SEQ = 8192
DEPTH = 1

CHUNK = 64
SGU_CHUNK = 128
SGU_WIDTH = D_MODEL // 2
SGU_GROUPS = 4
SGU_GROUP_DIM = SGU_WIDTH // SGU_GROUPS
MLSTM_WIDTH = D_MODEL // 2
MLSTM_HEADS = 4
MLSTM_V_DIM = MLSTM_WIDTH // MLSTM_HEADS
MLSTM_QK_DIM = MLSTM_V_DIM // 2
QK_TOTAL = MLSTM_HEADS * MLSTM_QK_DIM
CONV_WIDTH = 4
MIX_WIDTH = SGU_WIDTH + MLSTM_WIDTH
IN_WIDTH = 2 * SGU_WIDTH + 2 * QK_TOTAL + 2 * MLSTM_WIDTH + 2 * MLSTM_HEADS
D_FF = ((8 * D_MODEL // 3 + 127) // 128) * 128
EPS = 1e-6

kernel_name = "hybrid_sgu_mlstm_macaron_block"


def _rmsnorm(x, g):
    x32 = x.astype(jnp.float32)
    y = x32 * lax.rsqrt(jnp.mean(x32 * x32, axis=-1, keepdims=True) + EPS)
    return (y * g.astype(jnp.float32)).astype(x.dtype)


def _layernorm(x, g, b):
    x32 = x.astype(jnp.float32)
    mu = jnp.mean(x32, axis=-1, keepdims=True)
    xc = x32 - mu
    y = xc * lax.rsqrt(jnp.mean(xc * xc, axis=-1, keepdims=True) + EPS)
    return (y * g.astype(jnp.float32) + b.astype(jnp.float32)).astype(x.dtype)


def _swiglu(x, w_gate, w_up, w_down):
    return (jax.nn.silu(x @ w_gate) * (x @ w_up)) @ w_down


def _spatial_gating(z, ln_g, ln_b, w_s, b_s):
    bsz, seq, _ = z.shape
    u, v = z[..., :SGU_WIDTH], z[..., SGU_WIDTH:]
    v = _layernorm(v, ln_g, ln_b)
    vg = v.reshape(bsz, seq // SGU_CHUNK, SGU_CHUNK, SGU_GROUPS, SGU_GROUP_DIM)
    blk = jnp.arange(SGU_CHUNK) // CHUNK
    mask = blk[:, None] >= blk[None, :]
    w = jnp.where(mask[None], w_s, jnp.zeros_like(w_s))
    mixed = jnp.einsum('gts,bcsgd->bctgd', w, vg) + b_s.T[:, :, None]
    return u * mixed.reshape(bsz, seq, SGU_WIDTH)


def _causal_depthwise_conv(x, w, b):
    c = x.shape[-1]
    y = lax.conv_general_dilated(
        x, w[:, None, :], window_strides=(1,), padding=[(CONV_WIDTH - 1, 0)],
        dimension_numbers=('NWC', 'WIO', 'NWC'), feature_group_count=c)
    return y + b


def _mlstm_chunkwise(q, k, v, i_pre, f_pre):
    bsz, nh, seq, dk = q.shape
    dv = v.shape[-1]
    nc = seq // CHUNK
    q = q.astype(jnp.float32).reshape(bsz, nh, nc, CHUNK, dk)
    k = k.astype(jnp.float32).reshape(bsz, nh, nc, CHUNK, dk)
    v = v.astype(jnp.float32).reshape(bsz, nh, nc, CHUNK, dv)
    ig = i_pre.astype(jnp.float32).reshape(bsz, nh, nc, CHUNK)
    logf = jax.nn.log_sigmoid(f_pre.astype(jnp.float32)).reshape(bsz, nh, nc, CHUNK)
    b = jnp.cumsum(logf, axis=-1)
    b_last = b[..., -1]

    g = b_last[..., None] - b + ig
    m_loc = jnp.max(g, axis=-1)
    wg = jnp.exp(g - m_loc[..., None])
    c_loc = jnp.einsum('bhcl,bhclv,bhclk->bhcvk', wg, v, k)
    n_loc = jnp.einsum('bhcl,bhclk->bhck', wg, k)

    def step(carry, xs):
        c_st, n_st, m_st = carry
        c_l, n_l, m_l, bl = xs
        m_new = jnp.maximum(bl + m_st, m_l)
        a = jnp.exp(bl + m_st - m_new)
        e = jnp.exp(m_l - m_new)
        c_new = a[..., None, None] * c_st + e[..., None, None] * c_l
        n_new = a[..., None] * n_st + e[..., None] * n_l
        return (c_new, n_new, m_new), (c_st, n_st, m_st)

    init = (jnp.zeros((bsz, nh, dv, dk), jnp.float32),
            jnp.zeros((bsz, nh, dk), jnp.float32),
            jnp.zeros((bsz, nh), jnp.float32))
    xs = (jnp.moveaxis(c_loc, 2, 0), jnp.moveaxis(n_loc, 2, 0),
          jnp.moveaxis(m_loc, 2, 0), jnp.moveaxis(b_last, 2, 0))
    _, (c_prev, n_prev, m_prev) = lax.scan(step, init, xs)
    c_prev = jnp.moveaxis(c_prev, 0, 2)
    n_prev = jnp.moveaxis(n_prev, 0, 2)
    m_prev = jnp.moveaxis(m_prev, 0, 2)

    d = b[..., :, None] - b[..., None, :] + ig[..., None, :]
    tri = jnp.tril(jnp.ones((CHUNK, CHUNK), dtype=bool))
    d = jnp.where(tri, d, -jnp.inf)
    inter_log = b + m_prev[..., None]
    m_t = jnp.maximum(inter_log, jnp.max(d, axis=-1))
    w = jnp.exp(d - m_t[..., None]) * jnp.einsum('bhctd,bhcsd->bhcts', q, k)
    a_inter = jnp.exp(inter_log - m_t)
    num = (jnp.einsum('bhcts,bhcsv->bhctv', w, v)
           + a_inter[..., None] * jnp.einsum('bhcvk,bhctk->bhctv', c_prev, q))
    den = jnp.sum(w, axis=-1) + a_inter * jnp.einsum('bhck,bhctk->bhct', n_prev, q)
    h = num / jnp.maximum(jnp.abs(den), jnp.exp(-m_t))[..., None]
    return h.reshape(bsz, nh, seq, dv)


def _mlstm_mixer(zb, conv_w, conv_b, igate_b, fgate_b, mh_norm):
    bsz, seq, _ = zb.shape
    o0 = 2 * QK_TOTAL
    o1 = o0 + MLSTM_WIDTH
    o2 = o1 + MLSTM_WIDTH
    o3 = o2 + MLSTM_HEADS
    qk = jax.nn.silu(_causal_depthwise_conv(zb[..., :o0], conv_w, conv_b))
    v_raw = zb[..., o0:o1]
    o_raw = zb[..., o1:o2]
    i_raw = zb[..., o2:o3] + igate_b
    f_raw = zb[..., o3:] + fgate_b
    q = qk[..., :QK_TOTAL].reshape(bsz, seq, MLSTM_HEADS, MLSTM_QK_DIM).transpose(0, 2, 1, 3)
    k = (qk[..., QK_TOTAL:] * (MLSTM_QK_DIM ** -0.5)).reshape(
        bsz, seq, MLSTM_HEADS, MLSTM_QK_DIM).transpose(0, 2, 1, 3)
    v = v_raw.reshape(bsz, seq, MLSTM_HEADS, MLSTM_V_DIM).transpose(0, 2, 1, 3)
    h = _mlstm_chunkwise(q, k, v, i_raw.transpose(0, 2, 1), f_raw.transpose(0, 2, 1))
    h = h * lax.rsqrt(jnp.mean(h * h, axis=-1, keepdims=True) + EPS)
    h = h.transpose(0, 2, 1, 3).reshape(bsz, seq, MLSTM_WIDTH) * mh_norm.astype(jnp.float32)
    return (h * jax.nn.sigmoid(o_raw.astype(jnp.float32))).astype(zb.dtype)


def setup_inputs(seed: int = 0) -> dict:
    key = jax.random.key(seed)
    ks = jax.random.split(key, 24)

    def nrm(k, shape, scale):
        return jax.random.normal(k, shape, jnp.float32) * scale

    return {
        "x": nrm(ks[0], (BATCH, SEQ, D_MODEL), 1.0),
        "ffn1_norm": 1.0 + nrm(ks[1], (DEPTH, D_MODEL), 0.02),
        "ffn1_w_gate": nrm(ks[2], (DEPTH, D_MODEL, D_FF), D_MODEL ** -0.5),
        "ffn1_w_up": nrm(ks[3], (DEPTH, D_MODEL, D_FF), D_MODEL ** -0.5),
        "ffn1_w_down": nrm(ks[4], (DEPTH, D_FF, D_MODEL), D_FF ** -0.5),
        "mix_norm": 1.0 + nrm(ks[5], (DEPTH, D_MODEL), 0.02),
        "w_in": nrm(ks[6], (DEPTH, D_MODEL, IN_WIDTH), D_MODEL ** -0.5),
        "sgu_ln_g": 1.0 + nrm(ks[7], (DEPTH, SGU_WIDTH), 0.02),
        "sgu_ln_b": nrm(ks[8], (DEPTH, SGU_WIDTH), 0.02),
        "sgu_w": nrm(ks[9], (DEPTH, SGU_GROUPS, SGU_CHUNK, SGU_CHUNK), SGU_CHUNK ** -0.5),
        "sgu_b": 1.0 + nrm(ks[10], (DEPTH, SGU_GROUPS, SGU_CHUNK), 0.1),
        "conv_w": nrm(ks[11], (DEPTH, CONV_WIDTH, 2 * QK_TOTAL), CONV_WIDTH ** -0.5),
        "conv_b": nrm(ks[12], (DEPTH, 2 * QK_TOTAL), 0.02),
        "igate_b": nrm(ks[13], (DEPTH, MLSTM_HEADS), 0.1),
        "fgate_b": jnp.linspace(3.0, 6.0, MLSTM_HEADS, dtype=jnp.float32)[None, :]
                    + nrm(ks[14], (DEPTH, MLSTM_HEADS), 0.1),
        "mh_norm": 1.0 + nrm(ks[15], (DEPTH, MLSTM_WIDTH), 0.02),
        "w_out": nrm(ks[16], (DEPTH, MIX_WIDTH, D_MODEL), MIX_WIDTH ** -0.5),
        "ffn2_norm": 1.0 + nrm(ks[17], (DEPTH, D_MODEL), 0.02),
        "ffn2_w_gate": nrm(ks[18], (DEPTH, D_MODEL, D_FF), D_MODEL ** -0.5),
        "ffn2_w_up": nrm(ks[19], (DEPTH, D_MODEL, D_FF), D_MODEL ** -0.5),
        "ffn2_w_down": nrm(ks[20], (DEPTH, D_FF, D_MODEL), D_FF ** -0.5),
        "final_norm": 1.0 + nrm(ks[21], (D_MODEL,), 0.02),
    }


def reference(x, ffn1_norm, ffn1_w_gate, ffn1_w_up, ffn1_w_down, mix_norm, w_in,
              sgu_ln_g, sgu_ln_b, sgu_w, sgu_b, conv_w, conv_b, igate_b, fgate_b,
              mh_norm, w_out, ffn2_norm, ffn2_w_gate, ffn2_w_up, ffn2_w_down, final_norm):
    for l in range(DEPTH):
        h = _rmsnorm(x, ffn1_norm[l])
        x = x + 0.5 * _swiglu(h, ffn1_w_gate[l], ffn1_w_up[l], ffn1_w_down[l])
        h = _rmsnorm(x, mix_norm[l])
        z = h @ w_in[l]
        y_a = _spatial_gating(jax.nn.gelu(z[..., :2 * SGU_WIDTH]),
                              sgu_ln_g[l], sgu_ln_b[l], sgu_w[l], sgu_b[l])
        y_b = _mlstm_mixer(z[..., 2 * SGU_WIDTH:], conv_w[l], conv_b[l],
                           igate_b[l], fgate_b[l], mh_norm[l])
        x = x + jnp.concatenate([y_a, y_b], axis=-1) @ w_out[l]
        h = _rmsnorm(x, ffn2_norm[l])
        x = x + 0.5 * _swiglu(h, ffn2_w_gate[l], ffn2_w_up[l], ffn2_w_down[l])
    return _rmsnorm(x, final_norm)
```

```python
import math
import numpy as np
import concourse.bass as bass
import concourse.mybir as mybir
from concourse.bass_utils import run_bass_kernel_spmd

F32 = mybir.dt.float32
BF16 = mybir.dt.bfloat16
AF = mybir.ActivationFunctionType
ALU = mybir.AluOpType

D = 1024
DFF = 2816
NF = 22
NFH = 11
INW = 2568
NT = 512
NCH = 4
EPS = 1e-6
VS = 144
LN8 = math.log(8.0)


class _Op:
    __slots__ = ("eng", "fn", "deps", "signal", "sigval", "dma_key", "dma_val", "is_dma", "dma_total", "idx")


class Sched:
    ENGS = ("pe", "act", "dve", "pool", "sp")

    def __init__(self):
        self.ops = []
        self.last_w = {}
        self.readers = {}
        self.dma_counts = {}

    def add(self, eng, fn, reads=(), writes=(), dma=None, total=False):
        op = _Op()
        op.dma_total = total
        op.eng = eng
        op.fn = fn
        op.signal = False
        op.sigval = 0
        op.is_dma = dma is not None
        op.dma_key = dma
        op.dma_val = 0
        deps = []
        psr = tuple(r for r in reads if isinstance(r, tuple) and r[0] == "ps" and r not in writes)
        if psr:
            writes = tuple(writes) + psr
        raw_ids = set()
        for r in reads:
            w = self.last_w.get(r)
            if w is not None:
                deps.append(w)
                raw_ids.add(id(w))
        for w_ in writes:
            w = self.last_w.get(w_)
            if w is not None:
                deps.append(w)
            deps.extend(self.readers.get(w_, ()))
        for r in reads:
            self.readers.setdefault(r, []).append(op)
        for w_ in writes:
            self.last_w[w_] = op
            self.readers[w_] = []
        op.idx = len(self.ops)
        best = {}
        for d in deps:
            if d is op:
                continue
            same_eng = (d.eng == eng and eng in ("act", "dve", "pool"))
            if not (d.is_dma or d.eng != eng or same_eng):
                continue
            key = ("d", d.dma_key) if d.is_dma else ("e", d.eng)
            cur = best.get(key)
            if cur is None or d.idx > cur.idx:
                best[key] = d
        op.deps = list(best.values())
        for d in op.deps:
            if not d.is_dma:
                d.signal = True
        if op.is_dma:
            self.dma_counts[dma] = self.dma_counts.get(dma, 0) + 1
            op.dma_val = 16 * self.dma_counts[dma]
        self.ops.append(op)
        return op

    def emit(self, nc, final_waits):
        counts = {e: 0 for e in self.ENGS}
        for op in self.ops:
            if op.signal and not op.is_dma:
                counts[op.eng] += 1
                op.sigval = counts[op.eng]
        per_eng = {e: [op for op in self.ops if op.eng == e] for e in self.ENGS}
        import contextlib
        with contextlib.ExitStack() as st:
            esem = {e: st.enter_context(nc.semaphore("s_" + e)) for e in self.ENGS}
            dsem = {k: st.enter_context(nc.semaphore("d_%d" % i))
                    for i, k in enumerate(self.dma_counts)}
            block = st.enter_context(nc.Block())

            def run(engname, handle):
                waited = {}
                for op in per_eng[engname]:
                    for d in op.deps:
                        if d.is_dma:
                            val = 16 * self.dma_counts[d.dma_key] if d.dma_total else d.dma_val
                            sem, key = dsem[d.dma_key], ("d", d.dma_key)
                        else:
                            sem, val, key = esem[d.eng], d.sigval, ("e", d.eng)
                        if waited.get(key, 0) < val:
                            handle.wait_ge(sem, val)
                            waited[key] = val
                    ins = op.fn(handle)
                    if op.is_dma:
                        ins.then_inc(dsem[op.dma_key], 16)
                    elif op.signal:
                        ins.then_inc(esem[op.eng], 1)
                if engname == "sp":
                    for k in final_waits:
                        handle.wait_ge(dsem[k], 16 * self.dma_counts[k])

            @block.tensor
            def _(e):
                run("pe", e)

            @block.scalar
            def _(e):
                run("act", e)

            @block.vector
            def _(e):
                run("dve", e)

            @block.gpsimd
            def _(e):
                run("pool", e)

            @block.sync
            def _(e):
                run("sp", e)


def build_program(T, stop_after=9, skip=0, mstop=99, estop=99, debug=False):
    assert T % NT == 0
    ntiles = T // NT
    nc = bass.Bass("TRN2", target_bir_lowering=False)
    S = Sched()

    def din(name, shape):
        return nc.dram_tensor(name, list(shape), F32, kind="ExternalInput").ap()

    x_d = din("x", [T, D])
    norm_d = {n: din(n, [D]) for n in ("ffn1_norm", "mix_norm", "ffn2_norm", "final_norm")}
    wg_d = [din("ffn1_w_gate", [D, DFF]), din("ffn2_w_gate", [D, DFF])]
    wu_d = [din("ffn1_w_up", [D, DFF]), din("ffn2_w_up", [D, DFF])]
    wd_d = [din("ffn1_w_down", [DFF, D]), din("ffn2_w_down", [DFF, D])]
    win_d = din("w_in", [D, INW])
    lng_d = din("sgu_ln_g", [512])
    lnb_d = din("sgu_ln_b", [512])
    sguw_d = din("sgu_w", [4, 128, 128])
    sgub_d = din("sgu_b", [4, 128])
    convw_d = din("conv_w", [4, 512])
    convb_d = din("conv_b", [512])
    igb_d = din("igate_b", [4])
    fgb_d = din("fgate_b", [4])
    mhn_d = din("mh_norm", [512])
    wout_d = din("w_out", [D, D])
    out_d = nc.dram_tensor("out", [T, D], F32, kind="ExternalOutput").ap()

    NBLK = 29
    if _DBG.get("pad"):
        nc.dram_tensor("padscr", [_DBG["pad"], 128, 4096], BF16, kind="Internal")
    wsc = nc.dram_tensor("wsc", [NBLK, 128, 4096], BF16, kind="Internal").ap()
    wdsc = [nc.dram_tensor("wdsc%d" % n, [128, NF, D], BF16, kind="Internal").ap() for n in range(2)]
    wifsc = nc.dram_tensor("wifsc", [128, 8, 8], BF16, kind="Internal").ap()
    BLK_GU = [list(range(0, 11)), list(range(18, 29))]
    BLK_U, BLK_VS, BLK_QK, BLK_VM, BLK_O, BLK_WO0, BLK_WO1 = 11, 12, 13, 14, 15, 16, 17

    def sb(name, shape, dt=F32):
        return nc.alloc_sbuf_tensor(name, list(shape), dt).ap()

    xb = [sb("xb%d" % i, [128, NCH, D]) for i in range(2)]
    htok = [sb("htok%d" % i, [128, D], BF16) for i in range(4)]
    hT = sb("hT", [128, 8, NT], BF16)
    hTf = sb("hTf", [128, 8, NT], BF16)
    numsb = sb("numsb", [128, 2, 260])
    actb = sb("actb", [128, NFH, NT], BF16)
    sgs = [sb("sgs%d" % i, [128, NT]) for i in range(2)]
    NRING = 4
    ring = [sb("ring%d" % i, [128, 4096], BF16) for i in range(NRING)]
    wdb = sb("wdb", [128, NFH, D], BF16)
    gfin = sb("gfin", [128, D])
    junk = sb("junk", [128, D], BF16)
    ssq = sb("ssq", [128, 16])
    rstd = sb("rstd", [128, 16])
    gcol = sb("gcol", [128, 3, 8])
    zq = sb("zq", [128, 4, NT + 3])
    cacc = [sb("cacc%d" % i, [128, NT]) for i in range(2)]
    qkT = sb("qkT", [128, 4, NT], BF16)
    ktok = sb("ktok", [128, NCH, 256], BF16)
    qz = sb("qz", [128, 4, NT], BF16)
    vaug = sb("vaug", [128, NCH, 4, VS], BF16)
    ug = sb("ug", [128, 4, NT], BF16)
    vgs = [sb("vgs%d" % i, [128, NT]) for i in range(4)]
    vhat = [sb("vhat%d" % i, [128, NT], BF16) for i in range(2)]
    sigo = sb("sigo", [128, 4, NT], BF16)
    yT = sb("yT", [128, 8, NT], BF16)
    wTt = [sb("wTt%d" % i, [128, 4, 128], BF16) for i in range(2)]
    hn = [sb("hn%d" % i, [128, 4, 128], BF16) for i in range(2)]
    tmpA = sb("tmpA", [128, 4, 128])
    tmpB = sb("tmpB", [128, 4, 128])
    Rst = sb("Rst", [128, 2, VS])
    Sbf = [sb("Sbf%d" % i, [128, 2, VS], BF16) for i in range(2)]
    dcol = sb("dcol", [128, 2])
    ident = sb("ident", [128, 128], BF16)
    identf = sb("identf", [128, 128])
    tri = sb("tri", [128, 128])
    selA = sb("selA", [128, 128])
    selB = sb("selB", [128, 128])
    onesf = sb("onesf", [128, 128])
    WmT = sb("WmT", [128, 4, 128], BF16)
    T2 = sb("T2", [128, 4, 128])
    lngB = sb("lngB", [128, 4, 128])
    lngc = sb("lngc", [128, 4])
    lnbc = sb("lnbc", [128, 4])
    mhnc = sb("mhnc", [128, 4])
    cw = sb("cw", [128, 4, 4])
    cb = sb("cb", [128, 4])
    gbias = sb("gbias", [128, 8])
    wif = sb("wif", [128, 8, 8], BF16)
    gts = sb("gts", [128, NCH, 8])
    eft = sb("eft", [128, NCH, 4])
    lt = sb("lt", [128, NCH, 4])
    t1t = sb("t1t", [128, NCH, 4])
    kap = sb("kap", [128, NCH, 4])
    ebt = sb("ebt", [128, NCH, 4])
    dcn = sb("dcn", [128, NCH, 2])
    st1 = sb("st1", [128, 8])
    st2 = sb("st2", [128, 8])
    mu_t = sb("mu_t", [128, 4])
    var_t = sb("var_t", [128, 4])
    rs_t = sb("rs_t", [128, 4])
    nmr_t = sb("nmr_t", [128, 4])
    dabs = sb("dabs", [128, 4])
    rcp = sb("rcp", [128, 4])
    ssqn = sb("ssqn", [128, 4])
    sct = sb("sct", [128, 4])
    sc2 = sb("sc2", [128, 4])
    cst = sb("cst", [128, 4])

    bank = [nc.alloc_psum_tensor("bank%d" % i, [128, 512], F32).ap() for i in range(8)]
    bankb = [b_.bitcast(BF16) for b_ in bank]
    psb = bank[0:6]
    psT = bankb[6]
    ps7 = bank[7]
    PS = lambda i: ("ps", i)
    XR = lambda buf, c: ("xb", buf, c)
    XALL = lambda buf: tuple(("xb", buf, c) for c in range(NCH))
    HTALL = tuple(("hT", c) for c in range(NCH))
    HTFALL = tuple(("hTf", c) for c in range(NCH))

    def conv_dma(dst, src, wres, key):
        if skip & 1:
            return
        S.add("pool", lambda e, dst=dst, src=src: e.dma_start(out=dst, in_=src),
              reads=(), writes=(wres,), dma=key, total=True)

    def kview(w):
        return w.rearrange("(kc p) n -> p kc n", p=128)

    def conv_gu(n):
        for fp in range(NFH):
            b = BLK_GU[n][fp]
            conv_dma(wsc[b][:, 0:2048].rearrange("p (k m) -> p k m", k=8),
                     kview(wg_d[n])[:, :, fp * 256:(fp + 1) * 256], ("scr", b, 0), ("cvgu", n, fp if n == 0 else 0))
            conv_dma(wsc[b][:, 2048:4096].rearrange("p (k m) -> p k m", k=8),
                     kview(wu_d[n])[:, :, fp * 256:(fp + 1) * 256], ("scr", b, 1), ("cvgu", n, fp if n == 0 else 0))

    def conv_wd(n):
        wv = wd_d[n].rearrange("(f p) n -> p f n", p=128)
        for q, (f0, f1) in enumerate(((0, 6), (6, 11), (11, 17), (17, 22))):
            conv_dma(wdsc[n][:, f0:f1, :], wv[:, f0:f1, :], ("wdscr", n, q), ("cvd", n, q if n == 0 else 0))

    def conv_in(b, c0):
        conv_dma(wsc[b].rearrange("p (k m) -> p k m", k=8), kview(win_d)[:, :, c0:c0 + 512],
                 ("scr", b, 0), ("cvmix", b))

    S.add("sp", lambda e: e.dma_start(out=xb[0], in_=x_d[0:NT, :].rearrange("(c p) d -> p c d", p=128)),
          writes=XALL(0), dma=("xld", 0))

    def ld_const(dst, src, res, slow=False):
        if skip & 2:
            return
        S.add("sp", lambda e, dst=dst, src=src, slow=slow:
              e.dma_start(out=dst, in_=src, allow_slow_non_contiguous=slow),
              reads=(), writes=(res,), dma=("consts",), total=True)

    ld_const(gfin, norm_d["final_norm"].partition_broadcast(128), "gfin")
    for i, n in enumerate(("ffn1_norm", "mix_norm", "ffn2_norm")):
        ld_const(gcol[:, i, :], norm_d[n].rearrange("(k p) -> p k", p=128), ("gcol", i), slow=True)
    ld_const(lngc, lng_d.rearrange("(g p) -> p g", p=128), "lngc", slow=True)
    ld_const(lnbc, lnb_d.rearrange("(g p) -> p g", p=128), "lnbc", slow=True)
    ld_const(mhnc, mhn_d.rearrange("(g p) -> p g", p=128), "mhnc", slow=True)
    ld_const(cb, convb_d.rearrange("(j p) -> p j", p=128), "cb", slow=True)
    for k in range(4):
        ld_const(cw[:, :, k], convw_d[k].rearrange("(j p) -> p j", p=128), ("cw", k), slow=True)
    ld_const(gbias[:, 0:4], igb_d.partition_broadcast(128), "gbi")
    ld_const(gbias[:, 4:8], fgb_d.partition_broadcast(128), "gbf")
    wsg32 = xb[1][:, 0, 0:512].rearrange("p (g s) -> p g s", g=4)
    WmT32 = xb[1][:, 0, 512:1024].rearrange("p (g s) -> p g s", g=4)
    biasB = xb[1][:, 1, 0:512].rearrange("p (g s) -> p g s", g=4)
    XB1 = ("xb", 1, 0)
    XB1b = ("xb", 1, 1)
    S.add("sp", lambda e: e.dma_start(out=wsg32, in_=sguw_d.rearrange("g t s -> t g s")),
          writes=("wsg32",), dma=("consts",), total=True)
    S.add("sp", lambda e: e.dma_start(out=biasB, in_=sgub_d.partition_broadcast(128)),
          writes=("biasB",), dma=("consts",), total=True)

    S.add("pool", lambda e: e.memset(ident, 0.0), writes=("ident",))
    S.add("pool", lambda e: e.affine_select(out=ident, in_=ident, pattern=[[-1, 128]],
                                            compare_op=ALU.not_equal, fill=1.0, base=0,
                                            channel_multiplier=1), reads=("ident",), writes=("ident",))
    S.add("pool", lambda e: e.memset(identf, 0.0), writes=("identf",))
    S.add("pool", lambda e: e.affine_select(out=identf, in_=identf, pattern=[[-1, 128]],
                                            compare_op=ALU.not_equal, fill=1.0, base=0,
                                            channel_multiplier=1), reads=("identf",), writes=("identf",))
    S.add("pool", lambda e: e.memset(onesf, 1.0), writes=("onesf",))
    S.add("pool", lambda e: e.memset(cst[:, 0:1], 1.0), writes=("cst",))
    S.add("pool", lambda e: e.memset(cst[:, 1:2], -LN8), writes=("cst",))
    S.add("pool", lambda e: e.memset(cst[:, 2:3], EPS), writes=("cst",))
    S.add("pool", lambda e: e.memset(tri, 1.0), writes=("tri",))
    S.add("pool", lambda e: e.affine_select(out=tri, in_=tri, pattern=[[1, 128]],
                                            compare_op=ALU.is_ge, fill=0.0, base=0,
                                            channel_multiplier=-1), reads=("tri",), writes=("tri",))
    S.add("pool", lambda e: e.memset(selA, 0.0), writes=("selA",))
    S.add("pool", lambda e: e.memset(selA[:, 0:64], 1.0), writes=("selA",))
    S.add("pool", lambda e: e.memset(selB, 0.0), writes=("selB",))
    S.add("pool", lambda e: e.memset(selB[:, 64:128], 1.0), writes=("selB",))
    S.add("pool", lambda e: e.memset(zq, 0.0), writes=tuple(("zq", j) for j in range(4)))
    S.add("pool", lambda e: e.memset(Rst, 0.0), writes=("Rst",))
    for i_ in range(2):
        S.add("pool", lambda e, i_=i_: e.memset(Sbf[i_], 0.0), writes=(("Sbf", i_),))
    S.add("pool", lambda e: e.memset(dcol, 1.0), writes=("dcol",))
    S.add("pool", lambda e: e.memset(vaug, 0.0), writes=tuple(("vaug", c) for c in range(NCH)))
    S.add("pool", lambda e: e.memset(qz, 0.0), writes=tuple(("qz", h) for h in range(4)))

    for g in range(0 if not (skip & 4) else 4, 4):
        S.add("pe", lambda e, g=g: e.transpose(out=psb[4][:, g * 128:(g + 1) * 128],
                                               in_=wsg32[:, g, :], identity=identf),
              reads=(XB1, "wsg32", "identf"), writes=(PS(4),))
    S.add("dve", lambda e: e.tensor_copy(out=WmT32, in_=psb[4].rearrange("p (g s) -> p g s", g=4)),
          reads=(PS(4),), writes=(XB1,))
    S.add("dve", lambda e: e.memset(WmT32[64:128, :, 0:64], 0.0), reads=(XB1,), writes=(XB1,))
    S.add("dve", lambda e: e.tensor_copy(out=WmT, in_=WmT32), reads=(XB1,), writes=("WmT",))
    for g in range(4):
        S.add("pe", lambda e, g=g: e.matmul(psb[5][:, g * 128:(g + 1) * 128], lhsT=onesf,
                                            rhs=WmT32[:, g, :], start=True, stop=True),
              reads=(XB1, "onesf"), writes=(PS(5),))
    for g in range(4):
        S.add("dve", lambda e, g=g: e.scalar_tensor_tensor(
            out=T2[:, g, :], in0=psb[5][:, g * 128:(g + 1) * 128], scalar=lnbc[:, g:g + 1],
            in1=biasB[:, g, :], op0=ALU.mult, op1=ALU.add),
            reads=(PS(5), "lnbc", XB1b, "biasB"), writes=("T2",))
    S.add("dve", lambda e: e.tensor_copy(out=lngB, in_=lngc.unsqueeze(2).to_broadcast([128, 4, 128])),
          reads=("lngc",), writes=("lngB",))

    def conv_gu_blocks(n, fps):
        for fp in fps:
            b = BLK_GU[n][fp]
            conv_dma(wsc[b][:, 0:2048].rearrange("p (k m) -> p k m", k=8),
                     kview(wg_d[n])[:, :, fp * 256:(fp + 1) * 256], ("scr", b, 0), ("cvgu", n, fp if n == 0 else 0))
            conv_dma(wsc[b][:, 2048:4096].rearrange("p (k m) -> p k m", k=8),
                     kview(wu_d[n])[:, :, fp * 256:(fp + 1) * 256], ("scr", b, 1), ("cvgu", n, fp if n == 0 else 0))

    def conv_wd_parts(n, qs):
        wv = wd_d[n].rearrange("(f p) n -> p f n", p=128)
        for q in qs:
            f0, f1 = ((0, 6), (6, 11), (11, 17), (17, 22))[q]
            conv_dma(wdsc[n][:, f0:f1, :], wv[:, f0:f1, :], ("wdscr", n, q), ("cvd", n, q if n == 0 else 0))

    conv_gu_blocks(0, range(0, 3))
    conv_wd_parts(0, (0, 1))
    conv_gu_blocks(0, range(3, 8))
    conv_wd_parts(0, (2, 3))
    conv_gu_blocks(0, range(8, 11))
    conv_dma(wifsc, kview(win_d)[:, :, 2560:2568], ("wifscr",), ("cvmix",))
    conv_in(BLK_QK, 1024)
    conv_in(BLK_VM, 1536)
    conv_in(BLK_U, 0)
    conv_in(BLK_VS, 512)
    conv_in(BLK_O, 2048)
    for half, b in enumerate((BLK_WO0, BLK_WO1)):
        conv_dma(wsc[b].rearrange("p (k m) -> p k m", k=8),
                 wout_d.rearrange("(kc p) n -> p kc n", p=128)[:, :, half * 512:(half + 1) * 512],
                 ("scr", b, 0), ("cvmix", b))
    late_conv = []
    for fp in range(NFH):
        late_conv.append(lambda fp=fp: conv_gu_blocks(1, (fp,)))
        if fp in (2, 3):
            late_conv.append(lambda fp=fp: conv_wd_parts(1, (2 * (fp - 2), 2 * (fp - 2) + 1)))

    def drain_conv(k=1):
        for _ in range(k):
            if late_conv:
                late_conv.pop(0)()

    ring_ctr = [0]

    def ring_load(b):
        s = ring_ctr[0] % NRING
        ring_ctr[0] += 1
        S.add("sp", lambda e, s=s, b=b: e.dma_start(out=ring[s], in_=wsc[b]),
              reads=(("scr", b, 0), ("scr", b, 1)), writes=(("ring", s),), dma=("ring", s))
        return s

    plan = []
    plan += BLK_GU[0][0:6]
    for i in range(ntiles):
        plan += BLK_GU[0][5:11] + [BLK_QK, BLK_U, BLK_VS, BLK_VM, BLK_O]
        if i + 1 < ntiles:
            plan += BLK_GU[0][0:6]
        plan += [BLK_WO0, BLK_WO1] + BLK_GU[1][0:6] + BLK_GU[1][5:11]
    plan_pos = [0]
    issued = []

    def prefetch(depth):
        while len(issued) < depth and plan_pos[0] < len(plan):
            issued.append((plan[plan_pos[0]], ring_load(plan[plan_pos[0]])))
            plan_pos[0] += 1

    def next_block(expect):
        prefetch(1)
        b, s = issued.pop(0)
        assert b == expect, (b, expect)
        return s

    def load_wd(n, half):
        fl = (0, 6, 11) if half == 0 else (11, 17, 22)
        for q in range(2):
            f0, f1 = fl[q], fl[q + 1]
            S.add("sp", lambda e, n=n, f0=f0, f1=f1, half=half:
                  e.dma_start(out=wdb[:, f0 - 11 * half:f1 - 11 * half, :], in_=wdsc[n][:, f0:f1, :]),
                  reads=(("wdscr", n, 2 * half + q),), writes=(("wdb", q),), dma=("wdb", q))

    def load_x(i):
        buf = i % 2
        src = x_d[i * NT:(i + 1) * NT, :].rearrange("(c p) d -> p c d", p=128)
        S.add("sp", lambda e, buf=buf, src=src: e.dma_start(out=xb[buf], in_=src),
              writes=XALL(buf), dma=("xld", buf))

    def rsqrt_cols(src, dst, scale, rres, wres):
        S.add("dve", lambda e: e.tensor_scalar(out=dst, in0=src, scalar1=scale, scalar2=EPS,
                                               op0=ALU.mult, op1=ALU.add), reads=rres, writes=wres)
        S.add("act", lambda e: e.activation(out=dst, in_=dst, func=AF.Sqrt), reads=wres, writes=wres)
        S.add("dve", lambda e: e.reciprocal(out=dst, in_=dst), reads=wres, writes=wres)

    class Pipe:
        def __init__(self):
            self.items = []

        def at(self, step, fn):
            self.items.append((step, fn))

        def tick(self, step):
            for it in [it for it in self.items if it[0] <= step]:
                self.items.remove(it)
                it[1]()

        def flush(self):
            self.tick(10 ** 9)

    def stats_a(buf, c, col):
        xt = xb[buf]
        X = XR(buf, c)
        S.add("act", lambda e: e.activation(out=junk, in_=xt[:, c, :], func=AF.Square,
                                            accum_out=ssq[:, col:col + 1]),
              reads=(X,), writes=("junk", ("ssq", col)))
        S.add("act", lambda e: e.activation(out=rstd[:, col:col + 1], in_=ssq[:, col:col + 1], func=AF.Sqrt,
                                            scale=1.0 / D, bias=cst[:, 2:3]),
              reads=(("ssq", col), "cst"), writes=(("rstd", col),))

    def h_div(buf, c, col):
        xt = xb[buf]
        S.add("dve", lambda e: e.reciprocal(out=rstd[:, col:col + 1], in_=rstd[:, col:col + 1]),
              reads=(("rstd", col),), writes=(("rstd", col),))
        S.add("dve", lambda e: e.tensor_scalar(out=htok[c], in0=xt[:, c, :], scalar1=rstd[:, col:col + 1],
                                               scalar2=None, op0=ALU.mult),
              reads=(XR(buf, c), ("rstd", col)), writes=(("htok", c),))

    def h_T(c, gi, dst=None, dname="hT"):
        dst = hT if dst is None else dst
        hb = htok[c]
        HB = ("htok", c)
        pt = bankb[6 + c % 2]
        PT = PS(6 + c % 2)
        for k in range(8):
            S.add("pe", lambda e, k=k: e.transpose(out=pt[:, k * 128:(k + 1) * 128],
                                                   in_=hb[:, k * 128:(k + 1) * 128], identity=ident),
                  reads=(HB, "ident"), writes=(PT,))
        S.add("dve", lambda e: e.tensor_tensor(
            out=dst[:, :, c * 128:(c + 1) * 128], in0=pt.rearrange("p (k t) -> p k t", k=8),
            in1=gcol[:, gi, :].unsqueeze(2).to_broadcast([128, 8, 128]), op=ALU.mult),
            reads=(PT, ("gcol", gi)), writes=((dname, c),))

    def fin_scale(buf, c, col):
        xt = xb[buf]
        X = XR(buf, c)
        S.add("dve", lambda e: e.reciprocal(out=rstd[:, col:col + 1], in_=rstd[:, col:col + 1]),
              reads=(("rstd", col),), writes=(("rstd", col),))
        S.add("dve", lambda e: e.scalar_tensor_tensor(
            out=xt[:, c, :], in0=xt[:, c, :], scalar=rstd[:, col:col + 1], in1=gfin,
            op0=ALU.mult, op1=ALU.mult), reads=(X, ("rstd", col), "gfin"), writes=(X,))

    def norm_pipe(buf, gi, col0, lag=1, dst=None, dname="hT"):
        p = Pipe()
        for c in range(NCH):
            p.at(2 * c + lag, lambda c=c: stats_a(buf, c, col0 + c))
            p.at(2 * c + lag + 1, lambda c=c: h_div(buf, c, col0 + c))
            p.at(2 * c + lag + 2, lambda c=c: h_T(c, gi, dst, dname))
        return p

    def store_tile(i):
        buf = i % 2
        dst = out_d[i * NT:(i + 1) * NT, :].rearrange("(c p) d -> p c d", p=128)
        S.add("sp", lambda e: e.dma_start(out=dst, in_=xb[buf]),
              reads=XALL(buf), writes=(("outd", i),), dma=("xst", buf))

    def ffn_gu(i, n, half, src, srcres, overlap=False, hook=None):
        slot = None
        for fp in range(NFH):
            f0 = half * NFH + fp
            if f0 % 2 == 0 or fp == 0:
                slot = next_block(BLK_GU[n][f0 // 2])
                prefetch(3)
            if fp == 2:
                load_wd(n, half)
            if hook is not None and fp == 6:
                hook()
            f2 = f0 % 2
            RS = ("ring", slot)
            gi_ = f0 % 2
            ui_ = 2 if overlap else 2 + f0 % 2
            pg, pu = psb[gi_], psb[ui_]
            for k in range(8):
                S.add("pe", lambda e, k=k, slot=slot, f2=f2, pg=pg: e.matmul(
                    pg, lhsT=ring[slot][:, k * 256 + f2 * 128:k * 256 + f2 * 128 + 128],
                    rhs=src[:, k, :], start=(k == 0), stop=(k == 7)),
                    reads=(RS,) + srcres, writes=(PS(gi_),))
                if overlap and k == 3:
                    yield
            if overlap:
                yield
            for k in range(8):
                S.add("pe", lambda e, k=k, slot=slot, f2=f2, pu=pu: e.matmul(
                    pu, lhsT=ring[slot][:, 2048 + k * 256 + f2 * 128:2048 + k * 256 + f2 * 128 + 128],
                    rhs=src[:, k, :], start=(k == 0), stop=(k == 7)),
                    reads=(RS,) + srcres, writes=(PS(ui_),))
                if overlap and k == 3:
                    yield
            sg = sgs[f0 % 2]
            S.add("act", lambda e, pg=pg, sg=sg: e.activation(out=sg, in_=pg, func=AF.Silu),
                  reads=(PS(gi_),), writes=(("sgs", f0 % 2),))
            S.add("dve", lambda e, pu=pu, sg=sg, fp=fp: e.tensor_tensor(
                out=actb[:, fp, :], in0=sg, in1=pu, op=ALU.mult),
                reads=(PS(ui_), ("sgs", f0 % 2)), writes=(("actb", fp),))
            yield

    def ffn_down(i, n, half, pipe=None):
        buf = i % 2
        xt = xb[buf]
        for c in range(NCH):
            for dh in range(2):
                pi = 4 + (2 * c + dh) % 2
                po = psb[pi]
                for fp in range(NFH):
                    S.add("pe", lambda e, fp=fp, c=c, dh=dh, po=po: e.matmul(
                        po, lhsT=actb[:, fp, c * 128:(c + 1) * 128],
                        rhs=wdb[:, fp, dh * 512:(dh + 1) * 512],
                        start=(fp == 0), stop=(fp == NFH - 1)),
                        reads=(("actb", fp), ("wdb", 0 if fp < 6 else 1)), writes=(PS(pi),))
                S.add("dve", lambda e, c=c, dh=dh, po=po: e.scalar_tensor_tensor(
                    out=xt[:, c, dh * 512:(dh + 1) * 512], in0=po, scalar=0.5,
                    in1=xt[:, c, dh * 512:(dh + 1) * 512], op0=ALU.mult, op1=ALU.add),
                    reads=(PS(pi), XR(buf, c)), writes=(XR(buf, c),))
                if pipe is not None:
                    pipe.tick(2 * c + dh)
        if pipe is not None:
            pipe.flush()

    def run(gen):
        for _ in gen:
            pass

    def mixer(i, pipe=None, pre_a=None, pre_e=None, inter=None):
        buf = i % 2
        xt = xb[buf]

        def pull(k):
            if inter is not None:
                for _ in range(k):
                    next(inter, None)

        def gates():
            for c in range(NCH):
                for k in range(8):
                    S.add("pe", lambda e, c=c, k=k: e.matmul(
                        ps7[:, c * 8:(c + 1) * 8], lhsT=hT[:, k, c * 128:(c + 1) * 128], rhs=wif[:, k, :],
                        start=(k == 0), stop=(k == 7)), reads=(("hT", c), "wif"), writes=(PS(7),))
            S.add("dve", lambda e: e.tensor_tensor(
                out=gts, in0=ps7[:, 0:32].rearrange("p (c g) -> p c g", c=NCH),
                in1=gbias.unsqueeze(1).to_broadcast([128, NCH, 8]), op=ALU.add),
                reads=(PS(7), "gbi", "gbf"), writes=("gts",))
            S.add("act", lambda e: e.activation(out=eft, in_=gts[:, :, 4:8], func=AF.Exp, scale=-1.0),
                  reads=("gts",), writes=("eft",))
            S.add("act", lambda e: e.activation(out=lt, in_=eft, func=AF.Ln, bias=cst[:, 0:1]),
                  reads=("eft", "cst"), writes=("lt",))

        def gates_b():
            for c in range(NCH):
                S.add("pe", lambda e, c=c: e.matmul(ps7[:, 32 + c * 4:36 + c * 4], lhsT=tri, rhs=lt[:, c, :],
                                                    start=True, stop=True),
                      reads=("tri", "lt"), writes=(PS(7),))
                S.add("pe", lambda e, c=c: e.matmul(ps7[:, 48 + c * 2:50 + c * 2], lhsT=selA,
                                                    rhs=lt[:, c, 0:4:2], start=True, stop=False),
                      reads=("selA", "lt"), writes=(PS(7),))
                S.add("pe", lambda e, c=c: e.matmul(ps7[:, 48 + c * 2:50 + c * 2], lhsT=selB,
                                                    rhs=lt[:, c, 1:4:2], start=False, stop=True),
                      reads=("selB", "lt"), writes=(PS(7),))
            bpv = ps7[:, 32:48].rearrange("p (c h) -> p c h", c=NCH)
            S.add("dve", lambda e: e.tensor_tensor(out=t1t, in0=bpv, in1=gts[:, :, 0:4], op=ALU.add),
                  reads=(PS(7), "gts"), writes=("t1t",))
            S.add("act", lambda e: e.activation(out=ebt, in_=bpv, func=AF.Exp),
                  reads=(PS(7),), writes=("ebt",))
            S.add("act", lambda e: e.activation(out=dcn, in_=ps7[:, 48:56].rearrange("p (c j) -> p c j", c=NCH),
                                                func=AF.Exp, scale=-1.0),
                  reads=(PS(7),), writes=("dcn",))
            S.add("act", lambda e: e.activation(out=kap, in_=t1t, func=AF.Exp, bias=cst[:, 1:2]),
                  reads=("t1t", "cst"), writes=("kap",))

        def qk_proj(j, slot):
            RS = ("ring", slot)
            pz = psb[j % 2]
            for k in range(8):
                S.add("pe", lambda e, k=k: e.matmul(
                    pz, lhsT=ring[slot][:, k * 512 + j * 128:k * 512 + (j + 1) * 128], rhs=hT[:, k, :],
                    start=(k == 0), stop=(k == 7)), reads=(RS,) + HTALL, writes=(PS(j % 2),))
            ZQ = ("zq", j)
            S.add("act", lambda e: e.activation(out=zq[:, j, 3:NT + 3], in_=pz, func=AF.Copy),
                  reads=(PS(j % 2),), writes=(ZQ,))
            ca = cacc[j % 2]
            CA = ("cacc", j % 2)
            S.add("pool", lambda e: e.tensor_scalar(
                out=ca, in0=zq[:, j, 0:NT], scalar1=cw[:, j, 0:1], scalar2=cb[:, j:j + 1],
                op0=ALU.mult, op1=ALU.add), reads=(ZQ, ("cw", 0), "cb"), writes=(CA,))
            for tp in range(1, 4):
                S.add("dve", lambda e, tp=tp: e.scalar_tensor_tensor(
                    out=ca, in0=zq[:, j, tp:NT + tp], scalar=cw[:, j, tp:tp + 1], in1=ca,
                    op0=ALU.mult, op1=ALU.add), reads=(ZQ, ("cw", tp), CA), writes=(CA,))
            S.add("pool", lambda e: e.tensor_copy(out=zq[:, j, 0:3], in_=zq[:, j, NT:NT + 3]),
                  reads=(ZQ,), writes=(ZQ,))

        def qk_silu(j):
            ca = cacc[j % 2]
            CA = ("cacc", j % 2)
            if j < 2:
                S.add("act", lambda e: e.activation(out=qz[0:64, 2 * j, :], in_=ca[0:64, :], func=AF.Silu),
                      reads=(CA,), writes=(("qz", 2 * j),))
                S.add("act", lambda e: e.activation(out=qz[64:128, 2 * j + 1, :], in_=ca[64:128, :],
                                                    func=AF.Silu), reads=(CA,), writes=(("qz", 2 * j + 1),))
            else:
                S.add("act", lambda e: e.activation(out=qkT[:, j, :], in_=ca, func=AF.Silu),
                      reads=(CA,), writes=(("qkT", j),))

        def k_tr(c):
            for jj in range(2):
                S.add("pe", lambda e, jj=jj: e.transpose(
                    out=psT[:, jj * 128:(jj + 1) * 128], in_=qkT[:, 2 + jj, c * 128:(c + 1) * 128],
                    identity=ident), reads=(("qkT", 2 + jj), "ident"), writes=(PS(6),))
            S.add("act", lambda e: e.activation(out=ktok[:, c, :], in_=psT[:, 0:256], func=AF.Copy),
                  reads=(PS(6),), writes=(("ktok", c),))

        def vm_proj(c, slot):
            RS = ("ring", slot)
            vb = (4, 5, 2, 3)[c]
            pz = psb[vb]
            for k in range(8):
                S.add("pe", lambda e, k=k: e.matmul(
                    pz, lhsT=hT[:, k, c * 128:(c + 1) * 128], rhs=ring[slot][:, k * 512:(k + 1) * 512],
                    start=(k == 0), stop=(k == 7)), reads=(RS, ("hT", c)), writes=(PS(vb),))
            S.add("dve", lambda e: e.tensor_tensor(
                out=vaug[:, c, :, 0:128], in0=pz.rearrange("p (h v) -> p h v", h=4),
                in1=kap[:, c, :].unsqueeze(2).to_broadcast([128, 4, 128]), op=ALU.mult),
                reads=(PS(vb), "kap"), writes=(("vaug", c),))
            S.add("pool", lambda e: e.tensor_copy(out=vaug[:, c, :, 128:129], in_=kap[:, c, :].unsqueeze(2)),
                  reads=("kap",), writes=(("vaug", c),))

        def u_proj(g, slot):
            RS = ("ring", slot)
            pz = psb[g % 2]
            for k in range(8):
                S.add("pe", lambda e, k=k: e.matmul(
                    pz, lhsT=ring[slot][:, k * 512 + g * 128:k * 512 + (g + 1) * 128], rhs=hT[:, k, :],
                    start=(k == 0), stop=(k == 7)), reads=(RS,) + HTALL, writes=(PS(g % 2),))
            S.add("act", lambda e: e.activation(out=ug[:, g, :], in_=pz, func=AF.Gelu),
                  reads=(PS(g % 2),), writes=(("ug", g),))

        def vs_proj(c, slot):
            RS = ("ring", slot)
            pz = psb[2 + c % 2]
            for k in range(8):
                S.add("pe", lambda e, k=k: e.matmul(
                    pz, lhsT=hT[:, k, c * 128:(c + 1) * 128], rhs=ring[slot][:, k * 512:(k + 1) * 512],
                    start=(k == 0), stop=(k == 7)), reads=(RS, ("hT", c)), writes=(PS(2 + c % 2),))
            vg = vgs[c]
            VG = ("vgs", c)
            S.add("act", lambda e: e.activation(out=vg, in_=pz, func=AF.Gelu, accum_out=st1[:, c:c + 1]),
                  reads=(PS(2 + c % 2),), writes=(VG, ("st1", c)))
            S.add("act", lambda e: e.activation(out=junk[:, 0:NT], in_=vg, func=AF.Square,
                                                accum_out=st2[:, c:c + 1]),
                  reads=(VG,), writes=("junk", ("st2", c)))

        def ln_stats():
            ST = ("sgst",)
            st1r = tuple(("st1", c) for c in range(NCH))
            st2r = tuple(("st2", c) for c in range(NCH))
            S.add("dve", lambda e: e.tensor_scalar(out=mu_t, in0=st1[:, 0:4], scalar1=1.0 / 512, scalar2=None,
                                                   op0=ALU.mult), reads=st1r, writes=(ST,))
            S.add("dve", lambda e: e.tensor_tensor(out=var_t, in0=mu_t, in1=mu_t, op=ALU.mult),
                  reads=(ST,), writes=(ST,))
            S.add("dve", lambda e: e.scalar_tensor_tensor(out=var_t, in0=st2[:, 0:4], scalar=1.0 / 512, in1=var_t,
                                                          op0=ALU.mult, op1=ALU.subtract),
                  reads=(ST,) + st2r, writes=(ST,))
            S.add("act", lambda e: e.activation(out=rs_t, in_=var_t, func=AF.Sqrt, bias=cst[:, 2:3]),
                  reads=(ST, "cst"), writes=(ST,))
            S.add("dve", lambda e: e.reciprocal(out=rs_t, in_=rs_t), reads=(ST,), writes=(ST,))
            S.add("dve", lambda e: e.scalar_tensor_tensor(out=nmr_t, in0=mu_t, scalar=-1.0, in1=rs_t,
                                                          op0=ALU.mult, op1=ALU.mult), reads=(ST,), writes=(ST,))

        def sgu_chunk(c):
            ST = ("sgst",)
            vg = vgs[c]
            VG = ("vgs", c)
            vh = vhat[c % 2]
            VH = ("vhat", c % 2)
            S.add("pool", lambda e: e.tensor_scalar(
                out=vh, in0=vg, scalar1=rs_t[:, c:c + 1], scalar2=nmr_t[:, c:c + 1],
                op0=ALU.mult, op1=ALU.add), reads=(VG, ST), writes=(VH,))
            pull(1)
            for g in range(4):
                S.add("pe", lambda e, g=g: e.matmul(
                    bank[6][:, g * 128:(g + 1) * 128], lhsT=vh[:, g * 128:(g + 1) * 128], rhs=WmT[:, g, :],
                    start=True, stop=True), reads=(VH, "WmT"), writes=(PS(6),))
            S.add("dve", lambda e: e.tensor_tensor(
                out=tmpA, in0=bank[6].rearrange("p (g t) -> p g t", g=4), in1=lngB, op=ALU.mult),
                reads=(PS(6), "lngB"), writes=("tmpA",))
            S.add("pool", lambda e: e.tensor_tensor(out=tmpB, in0=tmpA, in1=T2, op=ALU.add),
                  reads=("tmpA", "T2"), writes=("tmpB",))
            S.add("pool", lambda e: e.tensor_tensor(
                out=yT[:, 0:4, c * 128:(c + 1) * 128], in0=tmpB, in1=ug[:, :, c * 128:(c + 1) * 128],
                op=ALU.mult), reads=("tmpB",) + tuple(("ug", g) for g in range(4)), writes=(("yTa", c),))

        def o_proj(jo, slot):
            RS = ("ring", slot)
            pz = psb[jo % 2]
            for k in range(8):
                S.add("pe", lambda e, k=k: e.matmul(
                    pz, lhsT=ring[slot][:, k * 512 + jo * 128:k * 512 + (jo + 1) * 128], rhs=hT[:, k, :],
                    start=(k == 0), stop=(k == 7)), reads=(RS,) + HTALL, writes=(PS(jo % 2),))
            S.add("act", lambda e: e.activation(out=sigo[:, jo, :], in_=pz, func=AF.Sigmoid),
                  reads=(PS(jo % 2),), writes=(("sigo", jo),))

        QZ = tuple(("qz", h) for h in range(4))

        def e_a(c):
            tcs = slice(c * 128, (c + 1) * 128)
            pull(1)
            for h in range(4):
                S.add("pe", lambda e, h=h: e.matmul(
                    psb[4][:, h * 128:(h + 1) * 128], lhsT=qkT[:, 2 + h // 2, tcs],
                    rhs=qz[:, h, tcs], start=True, stop=True),
                    reads=(("qkT", 2 + h // 2), ("qz", h)), writes=(PS(4),))
            wt = wTt[c % 2]
            WT = ("wTt", c % 2)
            S.add("dve", lambda e: e.tensor_tensor(
                out=wt, in0=psb[4].rearrange("p (h t) -> p h t", h=4),
                in1=tri.unsqueeze(1).to_broadcast([128, 4, 128]), op=ALU.mult),
                reads=(PS(4), "tri"), writes=(WT,))
            SB = ("Sbf", c % 2)
            S.add("pool", lambda e: e.tensor_tensor(
                out=Sbf[c % 2][:, :, 0:129], in0=Rst[:, :, 0:129],
                in1=dcol.unsqueeze(2).to_broadcast([128, 2, 129]), op=ALU.mult),
                reads=("Rst", "dcol"), writes=(SB,))
            for j in range(2):
                pull(1)
                for a in range(2):
                    h = 2 * j + a
                    S.add("pe", lambda e, h=h, j=j, a=a: e.matmul(
                        bank[5 + 2 * j][:, a * 130:a * 130 + 129], lhsT=ktok[:, c, j * 128:(j + 1) * 128],
                        rhs=vaug[:, c, h, 0:129], start=True, stop=True),
                        reads=(("ktok", c), ("vaug", c)), writes=(PS(5 + 2 * j),))
                for a in range(2):
                    p0 = a * 64
                    S.add("dve", lambda e, j=j, a=a, p0=p0: e.scalar_tensor_tensor(
                        out=Rst[p0:p0 + 64, j, 0:129], in0=Rst[p0:p0 + 64, j, 0:129],
                        scalar=dcol[p0:p0 + 64, j:j + 1],
                        in1=bank[5 + 2 * j][p0:p0 + 64, a * 130:a * 130 + 129],
                        op0=ALU.mult, op1=ALU.add),
                        reads=("Rst", "dcol", PS(5 + 2 * j)), writes=("Rst",))
            S.add("pool", lambda e: e.tensor_copy(out=dcol, in_=dcn[:, c, :]),
                  reads=("dcn",), writes=("dcol",))

        def e_b(c):
            tcs = slice(c * 128, (c + 1) * 128)
            wt = wTt[c % 2]
            WT = ("wTt", c % 2)
            SB = ("Sbf", c % 2)
            for j in range(2):
                pull(1)
                for a in range(2):
                    h = 2 * j + a
                    po = psb[3 + j][:, a * 130:a * 130 + 129]
                    S.add("pe", lambda e, h=h, po=po: e.matmul(
                        po, lhsT=wt[:, h, :], rhs=vaug[:, c, h, 0:129], start=True, stop=False),
                        reads=(WT, ("vaug", c)), writes=(PS(3 + j),))
                    S.add("pe", lambda e, h=h, j=j, po=po: e.matmul(
                        po, lhsT=qz[:, h, tcs], rhs=Sbf[c % 2][:, j, 0:129], start=False, stop=True),
                        reads=(("qz", h), SB), writes=(PS(3 + j),))
                S.add("act", lambda e, j=j: e.activation(out=numsb[:, j, :], in_=psb[3 + j][:, 0:260], func=AF.Copy),
                      reads=(PS(3 + j),), writes=(("numsb", j),))
            for j in range(2):
                pn = numsb[:, j, :].rearrange("p (a v) -> p a v", a=2)
                S.add("dve", lambda e, j=j, pn=pn: e.tensor_tensor(
                    out=dabs[:, 2 * j:2 * j + 2].unsqueeze(2), in0=pn[:, :, 128:129],
                    in1=ebt[:, c, 2 * j:2 * j + 2].unsqueeze(2), op=ALU.max),
                    reads=(("numsb", j), "ebt"), writes=(("dabs", j),))
                S.add("dve", lambda e, j=j, pn=pn: e.scalar_tensor_tensor(
                    out=dabs[:, 2 * j:2 * j + 2].unsqueeze(2), in0=pn[:, :, 128:129], scalar=-1.0,
                    in1=dabs[:, 2 * j:2 * j + 2].unsqueeze(2), op0=ALU.mult, op1=ALU.max),
                    reads=(("numsb", j), ("dabs", j)), writes=(("dabs", j),))
                for a in range(2):
                    h = 2 * j + a
                    S.add("act", lambda e, h=h, a=a, pn=pn: e.activation(
                        out=junk[:, 0:128], in_=pn[:, a, 0:128], func=AF.Square,
                        accum_out=ssqn[:, h:h + 1]), reads=(("numsb", j),), writes=("junk", ("ssqn", h)))
            HS = ("hstat",)
            S.add("dve", lambda e: e.reciprocal(out=rcp, in_=dabs),
                  reads=(("dabs", 0), ("dabs", 1)), writes=(HS,))
            S.add("dve", lambda e: e.tensor_tensor(out=sct, in0=rcp, in1=rcp, op=ALU.mult),
                  reads=(HS,), writes=(HS,))
            S.add("dve", lambda e: e.tensor_tensor(out=sct, in0=sct, in1=ssqn, op=ALU.mult),
                  reads=(HS,) + tuple(("ssqn", h) for h in range(4)), writes=(HS,))
            S.add("act", lambda e: e.activation(out=sct, in_=sct, func=AF.Sqrt, scale=1.0 / 128, bias=cst[:, 2:3]),
                  reads=(HS, "cst"), writes=(HS,))
            S.add("dve", lambda e: e.reciprocal(out=sct, in_=sct), reads=(HS,), writes=(HS,))
            S.add("dve", lambda e: e.tensor_tensor(out=sc2, in0=sct, in1=rcp, op=ALU.mult),
                  reads=(HS,), writes=(("sc2",),))
            hb = hn[c % 2]
            HN = ("hn", c % 2)
            for j in range(2):
                pn = numsb[:, j, :].rearrange("p (a v) -> p a v", a=2)
                S.add("dve", lambda e, j=j, pn=pn: e.tensor_tensor(
                    out=hb[:, 2 * j:2 * j + 2, :], in0=pn[:, :, 0:128],
                    in1=sc2[:, 2 * j:2 * j + 2].unsqueeze(2).to_broadcast([128, 2, 128]), op=ALU.mult),
                    reads=(("numsb", j), ("sc2",)), writes=(HN,))

        def e_c(c):
            tcs = slice(c * 128, (c + 1) * 128)
            hb = hn[c % 2]
            HN = ("hn", c % 2)
            pull(1)
            for h in range(4):
                S.add("pe", lambda e, h=h: e.transpose(
                    out=psT[:, h * 128:(h + 1) * 128], in_=hb[:, h, :], identity=ident),
                    reads=(HN, "ident"), writes=(PS(6),))
            for h in range(4):
                S.add("dve", lambda e, h=h: e.scalar_tensor_tensor(
                    out=yT[:, 4 + h, tcs], in0=psT[:, h * 128:(h + 1) * 128], scalar=mhnc[:, h:h + 1],
                    in1=sigo[:, h, tcs], op0=ALU.mult, op1=ALU.mult),
                    reads=(PS(6), "mhnc", ("sigo", h)), writes=(("yTb", c, h),))

        if pre_a is not None:
            pre_a()
        gates()
        slot = next_block(BLK_QK)
        prefetch(3)
        for j in range(4):
            qk_proj(j, slot)
            if j == 1:
                gates_b()
            if j >= 1:
                qk_silu(j - 1)
            drain_conv(1)
        qk_silu(3)
        slot = next_block(BLK_U)
        prefetch(3)
        for g in range(4):
            u_proj(g, slot)
            drain_conv(1)
        slot = next_block(BLK_VS)
        prefetch(3)
        for c in range(NCH):
            vs_proj(c, slot)
            drain_conv(1)
        ln_stats()
        slot = next_block(BLK_VM)
        prefetch(3)
        for c in range(NCH):
            vm_proj(c, slot)
            drain_conv(1)
        for c in range(NCH):
            k_tr(c)
        slot = next_block(BLK_O)
        prefetch(3)
        for jo in range(4):
            o_proj(jo, slot)
            drain_conv(1)
        if pre_e is not None:
            pre_e()

        for s_ in range(NCH + 2):
            if s_ < NCH:
                e_a(s_)
            pull(1)
            if 0 <= s_ - 1 < NCH:
                e_b(s_ - 1)
            if s_ < NCH:
                sgu_chunk(s_)
            pull(1)
            if 0 <= s_ - 2 < NCH:
                e_c(s_ - 2)
            drain_conv(1)
        pull(99)
        drain_conv(99)

        slots = (next_block(BLK_WO0), next_block(BLK_WO1))
        prefetch(2)
        for c in range(NCH):
            for dh in range(2):
                slot = slots[dh]
                RS = ("ring", slot)
                pi = (2 * c + dh) % 2
                po = psb[pi]
                for m in range(8):
                    S.add("pe", lambda e, m=m, c=c, slot=slot, po=po: e.matmul(
                        po, lhsT=yT[:, m, c * 128:(c + 1) * 128], rhs=ring[slot][:, m * 512:(m + 1) * 512],
                        start=(m == 0), stop=(m == 7)),
                        reads=(RS, ("yTa", c)) + tuple(("yTb", c, hh) for hh in range(4)), writes=(PS(pi),))
                S.add("dve", lambda e, c=c, dh=dh, po=po: e.tensor_tensor(
                    out=xt[:, c, dh * 512:(dh + 1) * 512], in0=po, in1=xt[:, c, dh * 512:(dh + 1) * 512],
                    op=ALU.add), reads=(PS(pi), XR(buf, c)), writes=(XR(buf, c),))
                if pipe is not None:
                    pipe.tick(2 * c + dh)

    norm_pipe(0, 0, 0, dst=hTf, dname="hTf").flush()
    prefetch(2)
    run(ffn_gu(0, 0, 0, hTf, HTFALL))
    S.add("sp", lambda e: e.dma_start(out=wif, in_=wifsc), reads=(("wifscr",),), writes=("wif",),
          dma=("cwif",))
    ffn_down(0, 0, 0)
    for i in range(ntiles):
        buf = i % 2
        more = i + 1 < ntiles
        nxt = (lambda i=i: load_x(i + 1)) if more else None
        run(ffn_gu(i, 0, 1, hTf, HTFALL, hook=nxt))
        ffn_down(i, 0, 1, pipe=norm_pipe(buf, 1, 4))
        inter = ffn_gu(i + 1, 0, 0, hTf, HTFALL, overlap=True) if more else None
        def pre_a(buf=buf):
            for c in range(NCH):
                stats_a(1 - buf, c, c)
            for c in range(NCH):
                h_div(1 - buf, c, c)

        def pre_b():
            for c in range(NCH):
                h_T(c, 0, hTf, "hTf")
        pm = norm_pipe(buf, 2, 8)
        mixer(i, pipe=pm, pre_a=pre_a if more else None, pre_e=pre_b if more else None, inter=inter)
        if more:
            rest = Pipe()
            for n_, it in enumerate(sorted(pm.items, key=lambda it: it[0])):
                rest.at(n_, it[1])
            ffn_down(i + 1, 0, 0, pipe=rest)
        else:
            pm.flush()
        p2 = Pipe()
        for c in range(NCH):
            p2.at(2 * c + 1, lambda c=c, buf=buf: stats_a(buf, c, 12 + c))
            p2.at(2 * c + 2, lambda c=c, buf=buf: fin_scale(buf, c, 12 + c))
        run(ffn_gu(i, 1, 0, hT, HTALL))
        ffn_down(i, 1, 0)
        run(ffn_gu(i, 1, 1, hT, HTALL))
        ffn_down(i, 1, 1, pipe=p2)
        store_tile(i)
    finals = [("xst", 0)] + ([("xst", 1)] if ntiles > 1 else [])
    if debug:
        dumps = {
            "yT": (yT, [("yTa", c) for c in range(4)] + [("yTb", c, h) for c in range(4) for h in range(4)]),
            "qz": (qz, [("qz", h) for h in range(4)]), "qkT": (qkT, [("qkT", 2), ("qkT", 3)]), "ug": (ug, [("ug", g) for g in range(4)]),
            "sigo": (sigo, [("sigo", g) for g in range(4)]), "vaug": (vaug, [("vaug", c) for c in range(4)]),
            "kap": (kap, ["kap"]), "ebt": (ebt, ["ebt"]), "dcn": (dcn, ["dcn"]), "gts": (gts, ["gts"]),
            "lt": (lt, ["lt"]), "ktok": (ktok, [("ktok", c) for c in range(4)]), "T2": (T2, ["T2"]),
            "WmT": (WmT, ["WmT"]), "Rst": (Rst, ["Rst"]), "hn1": (hn[1], [("hn", 1)]), "wT1": (wTt[1], [("wTt", 1)]),
            "zq": (zq, [("zq", j) for j in range(4)]), "vhat1": (vhat[1], [("vhat", 1)]),
            "tri": (tri, ["tri"]), "cw": (cw, [("cw", k) for k in range(4)]), "gcol": (gcol, [("gcol", k) for k in range(3)]),
            "hT": (hT, list(HTALL)), "st1": (st1, [("sgst",)]), "st2": (st2, [("sgst",)]), "mu_t": (mu_t, [("sgst",)]),
            "var_t": (var_t, [("sgst",)]), "rs_t": (rs_t, [("sgst",)]), "nmr_t": (nmr_t, [("sgst",)]), "eft": (eft, ["eft"]),
            "vgs3": (vgs[3], [("vgs", 3)]), "sc2": (sc2, [("sc2",)]), "rcp": (rcp, [("hstat",)]), "Sbf": (Sbf[0], [("Sbf", 0)]),
        }
        for name, (ap, rres) in dumps.items():
            dd = nc.dram_tensor("dbg_" + name, list(ap.shape), ap.dtype, kind="ExternalOutput").ap()
            S.add("sp", lambda e, dd=dd, ap=ap: e.dma_start(out=dd, in_=ap), reads=tuple(rres),
                  writes=(("dbgd", name),), dma=("dbg",), total=True)
        finals.append(("dbg",))
    S.emit(nc, finals)
    return nc


_DBG = dict(stop=9, skip=0, mstop=99, estop=99, debug=False, sink=None, pad=0)

_W_NAMES = ("ffn1_norm", "ffn1_w_gate", "ffn1_w_up", "ffn1_w_down", "mix_norm", "w_in", "sgu_ln_g",
            "sgu_ln_b", "sgu_w", "sgu_b", "conv_w", "conv_b", "igate_b", "fgate_b", "mh_norm", "w_out",
            "ffn2_norm", "ffn2_w_gate", "ffn2_w_up", "ffn2_w_down")


def kernel(**inputs):
    x = np.ascontiguousarray(np.asarray(inputs["x"], dtype=np.float32))
    B, T, _ = x.shape
    shared = {}
    for n in _W_NAMES:
        a = np.asarray(inputs[n], dtype=np.float32)
        shared[n] = np.ascontiguousarray(a[0])
    shared["final_norm"] = np.ascontiguousarray(np.asarray(inputs["final_norm"], dtype=np.float32))
    nc = build_program(T, _DBG['stop'], _DBG['skip'], _DBG['mstop'], _DBG['estop'], _DBG['debug'])
    in_maps = [dict(shared, x=x[b]) for b in range(B)]
    res = run_bass_kernel_spmd(nc, in_maps, core_ids=list(range(B)))
    if _DBG['sink'] is not None:
        _DBG['sink'](res.results)
    return np.stack([np.asarray(r["out"], dtype=np.float32) for r in res.results], axis=0)
```

```python
import math
import numpy as np
import concourse.bass as bass
import concourse.mybir as mybir
from concourse.bass_utils import run_bass_kernel_spmd

F32 = mybir.dt.float32
BF16 = mybir.dt.bfloat16
AF = mybir.ActivationFunctionType
ALU = mybir.AluOpType

D = 1024
DFF = 2816
NF = 22
NFH = 11
INW = 2568
NT = 512
NCH = 4
EPS = 1e-6
VS = 144
LN8 = math.log(8.0)


class _Op:
    __slots__ = ("eng", "fn", "deps", "signal", "sigval", "dma_key", "dma_val", "is_dma", "dma_total", "idx")


class Sched:
    ENGS = ("pe", "act", "dve", "pool", "sp")

    def __init__(self):
        self.ops = []
        self.last_w = {}
        self.readers = {}
        self.dma_counts = {}

    def add(self, eng, fn, reads=(), writes=(), dma=None, total=False):
        op = _Op()
        op.dma_total = total
        op.eng = eng
        op.fn = fn
        op.signal = False
        op.sigval = 0
        op.is_dma = dma is not None
        op.dma_key = dma
        op.dma_val = 0
        deps = []
        psr = tuple(r for r in reads if isinstance(r, tuple) and r[0] == "ps" and r not in writes)
        if psr:
            writes = tuple(writes) + psr
        raw_ids = set()
        for r in reads:
            w = self.last_w.get(r)
            if w is not None:
                deps.append(w)
                raw_ids.add(id(w))
        for w_ in writes:
            w = self.last_w.get(w_)
            if w is not None:
                deps.append(w)
            deps.extend(self.readers.get(w_, ()))
        for r in reads:
            self.readers.setdefault(r, []).append(op)
        for w_ in writes:
            self.last_w[w_] = op
            self.readers[w_] = []
        op.idx = len(self.ops)
        best = {}
        for d in deps:
            if d is op:
                continue
            same_eng = (d.eng == eng and eng in ("act", "dve", "pool"))
            if not (d.is_dma or d.eng != eng or same_eng):
                continue
            key = ("d", d.dma_key) if d.is_dma else ("e", d.eng)
            cur = best.get(key)
            if cur is None or d.idx > cur.idx:
                best[key] = d
        op.deps = list(best.values())
        for d in op.deps:
            if not d.is_dma:
                d.signal = True
        if op.is_dma:
            self.dma_counts[dma] = self.dma_counts.get(dma, 0) + 1
            op.dma_val = 16 * self.dma_counts[dma]
        self.ops.append(op)
        return op

    def emit(self, nc, final_waits):
        counts = {e: 0 for e in self.ENGS}
        for op in self.ops:
            if op.signal and not op.is_dma:
                counts[op.eng] += 1
                op.sigval = counts[op.eng]
        per_eng = {e: [op for op in self.ops if op.eng == e] for e in self.ENGS}
        import contextlib
        with contextlib.ExitStack() as st:
            esem = {e: st.enter_context(nc.semaphore("s_" + e)) for e in self.ENGS}
            dsem = {k: st.enter_context(nc.semaphore("d_%d" % i))
                    for i, k in enumerate(self.dma_counts)}
            block = st.enter_context(nc.Block())

            def run(engname, handle):
                waited = {}
                for op in per_eng[engname]:
                    for d in op.deps:
                        if d.is_dma:
                            val = 16 * self.dma_counts[d.dma_key] if d.dma_total else d.dma_val
                            sem, key = dsem[d.dma_key], ("d", d.dma_key)
                        else:
                            sem, val, key = esem[d.eng], d.sigval, ("e", d.eng)
                        if waited.get(key, 0) < val:
                            handle.wait_ge(sem, val)
                            waited[key] = val
                    ins = op.fn(handle)
                    if op.is_dma:
                        ins.then_inc(dsem[op.dma_key], 16)
                    elif op.signal:
                        ins.then_inc(esem[op.eng], 1)
                if engname == "sp":
                    for k in final_waits:
                        handle.wait_ge(dsem[k], 16 * self.dma_counts[k])

            @block.tensor
            def _(e):
                run("pe", e)

            @block.scalar
            def _(e):
                run("act", e)

            @block.vector
            def _(e):
                run("dve", e)

            @block.gpsimd
            def _(e):
                run("pool", e)

            @block.sync
            def _(e):
                run("sp", e)


def build_program(T, stop_after=9, skip=0, mstop=99, estop=99, debug=False):
    assert T % NT == 0
    ntiles = T // NT
    nc = bass.Bass("TRN2", target_bir_lowering=False)
    S = Sched()

    def din(name, shape):
        return nc.dram_tensor(name, list(shape), F32, kind="ExternalInput").ap()

    x_d = din("x", [T, D])
    norm_d = {n: din(n, [D]) for n in ("ffn1_norm", "mix_norm", "ffn2_norm", "final_norm")}
    wg_d = [din("ffn1_w_gate", [D, DFF]), din("ffn2_w_gate", [D, DFF])]
    wu_d = [din("ffn1_w_up", [D, DFF]), din("ffn2_w_up", [D, DFF])]
    wd_d = [din("ffn1_w_down", [DFF, D]), din("ffn2_w_down", [DFF, D])]
    win_d = din("w_in", [D, INW])
    lng_d = din("sgu_ln_g", [512])
    lnb_d = din("sgu_ln_b", [512])
    sguw_d = din("sgu_w", [4, 128, 128])
    sgub_d = din("sgu_b", [4, 128])
    convw_d = din("conv_w", [4, 512])
    convb_d = din("conv_b", [512])
    igb_d = din("igate_b", [4])
    fgb_d = din("fgate_b", [4])
    mhn_d = din("mh_norm", [512])
    wout_d = din("w_out", [D, D])
    out_d = nc.dram_tensor("out", [T, D], F32, kind="ExternalOutput").ap()

    NBLK = 29
    if _DBG.get("pad"):
        nc.dram_tensor("padscr", [_DBG["pad"], 128, 4096], BF16, kind="Internal")
    wsc = nc.dram_tensor("wsc", [NBLK, 128, 4096], BF16, kind="Internal").ap()
    wdsc = [nc.dram_tensor("wdsc%d" % n, [128, NF, D], BF16, kind="Internal").ap() for n in range(2)]
    wifsc = nc.dram_tensor("wifsc", [128, 8, 8], BF16, kind="Internal").ap()
    BLK_GU = [list(range(0, 11)), list(range(18, 29))]
    BLK_U, BLK_VS, BLK_QK, BLK_VM, BLK_O, BLK_WO0, BLK_WO1 = 11, 12, 13, 14, 15, 16, 17

    def sb(name, shape, dt=F32):
        return nc.alloc_sbuf_tensor(name, list(shape), dt).ap()

    xb = [sb("xb%d" % i, [128, NCH, D]) for i in range(2)]
    htok = [sb("htok%d" % i, [128, D], BF16) for i in range(4)]
    hT = sb("hT", [128, 8, NT], BF16)
    hTf = sb("hTf", [128, 8, NT], BF16)
    numsb = sb("numsb", [128, 2, 260])
    actb = sb("actb", [128, NFH, NT], BF16)
    sgs = [sb("sgs%d" % i, [128, NT]) for i in range(2)]
    NRING = 4
    ring = [sb("ring%d" % i, [128, 4096], BF16) for i in range(NRING)]
    wdb = sb("wdb", [128, NFH, D], BF16)
    gfin = sb("gfin", [128, D])
    junk = sb("junk", [128, D], BF16)
    ssq = sb("ssq", [128, 16])
    rstd = sb("rstd", [128, 16])
    gcol = sb("gcol", [128, 3, 8])
    zq = sb("zq", [128, 4, NT + 3])
    cacc = [sb("cacc%d" % i, [128, NT]) for i in range(2)]
    qkT = sb("qkT", [128, 4, NT], BF16)
    ktok = sb("ktok", [128, NCH, 256], BF16)
    qz = sb("qz", [128, 4, NT], BF16)
    vaug = sb("vaug", [128, NCH, 4, VS], BF16)
    ug = sb("ug", [128, 4, NT], BF16)
    vgs = [sb("vgs%d" % i, [128, NT]) for i in range(4)]
    vhat = [sb("vhat%d" % i, [128, NT], BF16) for i in range(2)]
    sigo = sb("sigo", [128, 4, NT], BF16)
    yT = sb("yT", [128, 8, NT], BF16)
    wTt = [sb("wTt%d" % i, [128, 4, 128], BF16) for i in range(2)]
    hn = [sb("hn%d" % i, [128, 4, 128], BF16) for i in range(2)]
    tmpA = sb("tmpA", [128, 4, 128])
    tmpB = sb("tmpB", [128, 4, 128])
    Rst = sb("Rst", [128, 2, VS])
    Sbf = [sb("Sbf%d" % i, [128, 2, VS], BF16) for i in range(2)]
    dcol = sb("dcol", [128, 2])
    ident = sb("ident", [128, 128], BF16)
    identf = sb("identf", [128, 128])
    tri = sb("tri", [128, 128])
    selA = sb("selA", [128, 128])
    selB = sb("selB", [128, 128])
    onesf = sb("onesf", [128, 128])
    WmT = sb("WmT", [128, 4, 128], BF16)
    T2 = sb("T2", [128, 4, 128])
    lngB = sb("lngB", [128, 4, 128])
    lngc = sb("lngc", [128, 4])
    lnbc = sb("lnbc", [128, 4])
    mhnc = sb("mhnc", [128, 4])
    cw = sb("cw", [128, 4, 4])
    cb = sb("cb", [128, 4])
    gbias = sb("gbias", [128, 8])
    wif = sb("wif", [128, 8, 8], BF16)
    gts = sb("gts", [128, NCH, 8])
    eft = sb("eft", [128, NCH, 4])
    lt = sb("lt", [128, NCH, 4])
    t1t = sb("t1t", [128, NCH, 4])
    kap = sb("kap", [128, NCH, 4])
    ebt = sb("ebt", [128, NCH, 4])
    dcn = sb("dcn", [128, NCH, 2])
    st1 = sb("st1", [128, 8])
    st2 = sb("st2", [128, 8])
    mu_t = sb("mu_t", [128, 4])
    var_t = sb("var_t", [128, 4])
    rs_t = sb("rs_t", [128, 4])
    nmr_t = sb("nmr_t", [128, 4])
    dabs = sb("dabs", [128, 4])
    rcp = sb("rcp", [128, 4])
    ssqn = sb("ssqn", [128, 4])
    sct = sb("sct", [128, 4])
    sc2 = sb("sc2", [128, 4])
    cst = sb("cst", [128, 4])

    bank = [nc.alloc_psum_tensor("bank%d" % i, [128, 512], F32).ap() for i in range(8)]
    bankb = [b_.bitcast(BF16) for b_ in bank]
    psb = bank[0:6]
    psT = bankb[6]
    ps7 = bank[7]
    PS = lambda i: ("ps", i)
    XR = lambda buf, c: ("xb", buf, c)
    XALL = lambda buf: tuple(("xb", buf, c) for c in range(NCH))
    HTALL = tuple(("hT", c) for c in range(NCH))
    HTFALL = tuple(("hTf", c) for c in range(NCH))

    def conv_dma(dst, src, wres, key):
        if skip & 1:
            return
        S.add("pool", lambda e, dst=dst, src=src: e.dma_start(out=dst, in_=src),
              reads=(), writes=(wres,), dma=key, total=True)

    def kview(w):
        return w.rearrange("(kc p) n -> p kc n", p=128)

    def conv_gu(n):
        for fp in range(NFH):
            b = BLK_GU[n][fp]
            conv_dma(wsc[b][:, 0:2048].rearrange("p (k m) -> p k m", k=8),
                     kview(wg_d[n])[:, :, fp * 256:(fp + 1) * 256], ("scr", b, 0), ("cvgu", n, fp if n == 0 else 0))
            conv_dma(wsc[b][:, 2048:4096].rearrange("p (k m) -> p k m", k=8),
                     kview(wu_d[n])[:, :, fp * 256:(fp + 1) * 256], ("scr", b, 1), ("cvgu", n, fp if n == 0 else 0))

    def conv_wd(n):
        wv = wd_d[n].rearrange("(f p) n -> p f n", p=128)
        for q, (f0, f1) in enumerate(((0, 6), (6, 11), (11, 17), (17, 22))):
            conv_dma(wdsc[n][:, f0:f1, :], wv[:, f0:f1, :], ("wdscr", n, q), ("cvd", n, q if n == 0 else 0))

    def conv_in(b, c0):
        conv_dma(wsc[b].rearrange("p (k m) -> p k m", k=8), kview(win_d)[:, :, c0:c0 + 512],
                 ("scr", b, 0), ("cvmix", b))

    S.add("sp", lambda e: e.dma_start(out=xb[0], in_=x_d[0:NT, :].rearrange("(c p) d -> p c d", p=128)),
          writes=XALL(0), dma=("xld", 0))

    def ld_const(dst, src, res, slow=False):
        if skip & 2:
            return
        S.add("sp", lambda e, dst=dst, src=src, slow=slow:
              e.dma_start(out=dst, in_=src, allow_slow_non_contiguous=slow),
              reads=(), writes=(res,), dma=("consts",), total=True)

    ld_const(gfin, norm_d["final_norm"].partition_broadcast(128), "gfin")
    for i, n in enumerate(("ffn1_norm", "mix_norm", "ffn2_norm")):
        ld_const(gcol[:, i, :], norm_d[n].rearrange("(k p) -> p k", p=128), ("gcol", i), slow=True)
    ld_const(lngc, lng_d.rearrange("(g p) -> p g", p=128), "lngc", slow=True)
    ld_const(lnbc, lnb_d.rearrange("(g p) -> p g", p=128), "lnbc", slow=True)
    ld_const(mhnc, mhn_d.rearrange("(g p) -> p g", p=128), "mhnc", slow=True)
    ld_const(cb, convb_d.rearrange("(j p) -> p j", p=128), "cb", slow=True)
    for k in range(4):
        ld_const(cw[:, :, k], convw_d[k].rearrange("(j p) -> p j", p=128), ("cw", k), slow=True)
    ld_const(gbias[:, 0:4], igb_d.partition_broadcast(128), "gbi")
    ld_const(gbias[:, 4:8], fgb_d.partition_broadcast(128), "gbf")
    wsg32 = xb[1][:, 0, 0:512].rearrange("p (g s) -> p g s", g=4)
    WmT32 = xb[1][:, 0, 512:1024].rearrange("p (g s) -> p g s", g=4)
    biasB = xb[1][:, 1, 0:512].rearrange("p (g s) -> p g s", g=4)
    XB1 = ("xb", 1, 0)
    XB1b = ("xb", 1, 1)
    S.add("sp", lambda e: e.dma_start(out=wsg32, in_=sguw_d.rearrange("g t s -> t g s")),
          writes=("wsg32",), dma=("consts",), total=True)
    S.add("sp", lambda e: e.dma_start(out=biasB, in_=sgub_d.partition_broadcast(128)),
          writes=("biasB",), dma=("consts",), total=True)

    S.add("pool", lambda e: e.memset(ident, 0.0), writes=("ident",))
    S.add("pool", lambda e: e.affine_select(out=ident, in_=ident, pattern=[[-1, 128]],
                                            compare_op=ALU.not_equal, fill=1.0, base=0,
                                            channel_multiplier=1), reads=("ident",), writes=("ident",))
    S.add("pool", lambda e: e.memset(identf, 0.0), writes=("identf",))
    S.add("pool", lambda e: e.affine_select(out=identf, in_=identf, pattern=[[-1, 128]],
                                            compare_op=ALU.not_equal, fill=1.0, base=0,
                                            channel_multiplier=1), reads=("identf",), writes=("identf",))
    S.add("pool", lambda e: e.memset(onesf, 1.0), writes=("onesf",))
    S.add("pool", lambda e: e.memset(cst[:, 0:1], 1.0), writes=("cst",))
    S.add("pool", lambda e: e.memset(cst[:, 1:2], -LN8), writes=("cst",))
    S.add("pool", lambda e: e.memset(cst[:, 2:3], EPS), writes=("cst",))
    S.add("pool", lambda e: e.memset(tri, 1.0), writes=("tri",))
    S.add("pool", lambda e: e.affine_select(out=tri, in_=tri, pattern=[[1, 128]],
                                            compare_op=ALU.is_ge, fill=0.0, base=0,
                                            channel_multiplier=-1), reads=("tri",), writes=("tri",))
    S.add("pool", lambda e: e.memset(selA, 0.0), writes=("selA",))
    S.add("pool", lambda e: e.memset(selA[:, 0:64], 1.0), writes=("selA",))
    S.add("pool", lambda e: e.memset(selB, 0.0), writes=("selB",))
    S.add("pool", lambda e: e.memset(selB[:, 64:128], 1.0), writes=("selB",))
    S.add("pool", lambda e: e.memset(zq, 0.0), writes=tuple(("zq", j) for j in range(4)))
    S.add("pool", lambda e: e.memset(Rst, 0.0), writes=("Rst",))
    for i_ in range(2):
        S.add("pool", lambda e, i_=i_: e.memset(Sbf[i_], 0.0), writes=(("Sbf", i_),))
    S.add("pool", lambda e: e.memset(dcol, 1.0), writes=("dcol",))
    S.add("pool", lambda e: e.memset(vaug, 0.0), writes=tuple(("vaug", c) for c in range(NCH)))
    S.add("pool", lambda e: e.memset(qz, 0.0), writes=tuple(("qz", h) for h in range(4)))

    for g in range(0 if not (skip & 4) else 4, 4):
        S.add("pe", lambda e, g=g: e.transpose(out=psb[4][:, g * 128:(g + 1) * 128],
                                               in_=wsg32[:, g, :], identity=identf),
              reads=(XB1, "wsg32", "identf"), writes=(PS(4),))
    S.add("dve", lambda e: e.tensor_copy(out=WmT32, in_=psb[4].rearrange("p (g s) -> p g s", g=4)),
          reads=(PS(4),), writes=(XB1,))
    S.add("dve", lambda e: e.memset(WmT32[64:128, :, 0:64], 0.0), reads=(XB1,), writes=(XB1,))
    S.add("dve", lambda e: e.tensor_copy(out=WmT, in_=WmT32), reads=(XB1,), writes=("WmT",))
    for g in range(4):
        S.add("pe", lambda e, g=g: e.matmul(psb[5][:, g * 128:(g + 1) * 128], lhsT=onesf,
                                            rhs=WmT32[:, g, :], start=True, stop=True),
              reads=(XB1, "onesf"), writes=(PS(5),))
    for g in range(4):
        S.add("dve", lambda e, g=g: e.scalar_tensor_tensor(
            out=T2[:, g, :], in0=psb[5][:, g * 128:(g + 1) * 128], scalar=lnbc[:, g:g + 1],
            in1=biasB[:, g, :], op0=ALU.mult, op1=ALU.add),
            reads=(PS(5), "lnbc", XB1b, "biasB"), writes=("T2",))
    S.add("dve", lambda e: e.tensor_copy(out=lngB, in_=lngc.unsqueeze(2).to_broadcast([128, 4, 128])),
          reads=("lngc",), writes=("lngB",))

    def conv_gu_blocks(n, fps):
        for fp in fps:
            b = BLK_GU[n][fp]
            conv_dma(wsc[b][:, 0:2048].rearrange("p (k m) -> p k m", k=8),
                     kview(wg_d[n])[:, :, fp * 256:(fp + 1) * 256], ("scr", b, 0), ("cvgu", n, fp if n == 0 else 0))
            conv_dma(wsc[b][:, 2048:4096].rearrange("p (k m) -> p k m", k=8),
                     kview(wu_d[n])[:, :, fp * 256:(fp + 1) * 256], ("scr", b, 1), ("cvgu", n, fp if n == 0 else 0))

    def conv_wd_parts(n, qs):
        wv = wd_d[n].rearrange("(f p) n -> p f n", p=128)
        for q in qs:
            f0, f1 = ((0, 6), (6, 11), (11, 17), (17, 22))[q]
            conv_dma(wdsc[n][:, f0:f1, :], wv[:, f0:f1, :], ("wdscr", n, q), ("cvd", n, q if n == 0 else 0))

    conv_gu_blocks(0, range(0, 3))
    conv_wd_parts(0, (0, 1))
    conv_gu_blocks(0, range(3, 8))
    conv_wd_parts(0, (2, 3))
    conv_gu_blocks(0, range(8, 11))
    conv_dma(wifsc, kview(win_d)[:, :, 2560:2568], ("wifscr",), ("cvmix",))
    conv_in(BLK_QK, 1024)
    conv_in(BLK_VM, 1536)
    conv_in(BLK_U, 0)
    conv_in(BLK_VS, 512)
    conv_in(BLK_O, 2048)
    for half, b in enumerate((BLK_WO0, BLK_WO1)):
        conv_dma(wsc[b].rearrange("p (k m) -> p k m", k=8),
                 wout_d.rearrange("(kc p) n -> p kc n", p=128)[:, :, half * 512:(half + 1) * 512],
                 ("scr", b, 0), ("cvmix", b))
    late_conv = []
    for fp in range(NFH):
        late_conv.append(lambda fp=fp: conv_gu_blocks(1, (fp,)))
        if fp in (2, 3):
            late_conv.append(lambda fp=fp: conv_wd_parts(1, (2 * (fp - 2), 2 * (fp - 2) + 1)))

    def drain_conv(k=1):
        for _ in range(k):
            if late_conv:
                late_conv.pop(0)()

    ring_ctr = [0]

    def ring_load(b):
        s = ring_ctr[0] % NRING
        ring_ctr[0] += 1
        S.add("sp", lambda e, s=s, b=b: e.dma_start(out=ring[s], in_=wsc[b]),
              reads=(("scr", b, 0), ("scr", b, 1)), writes=(("ring", s),), dma=("ring", s))
        return s

    plan = []
    plan += BLK_GU[0][0:6]
    for i in range(ntiles):
        plan += BLK_GU[0][5:11] + [BLK_QK, BLK_VM, BLK_U, BLK_VS, BLK_O]
        if i + 1 < ntiles:
            plan += BLK_GU[0][0:6]
        plan += [BLK_WO0, BLK_WO1] + BLK_GU[1][0:6] + BLK_GU[1][5:11]
    plan_pos = [0]
    issued = []

    def prefetch(depth):
        while len(issued) < depth and plan_pos[0] < len(plan):
            issued.append((plan[plan_pos[0]], ring_load(plan[plan_pos[0]])))
            plan_pos[0] += 1

    def next_block(expect):
        prefetch(1)
        b, s = issued.pop(0)
        assert b == expect, (b, expect)
        return s

    def load_wd(n, half):
        fl = (0, 6, 11) if half == 0 else (11, 17, 22)
        for q in range(2):
            f0, f1 = fl[q], fl[q + 1]
            S.add("sp", lambda e, n=n, f0=f0, f1=f1, half=half:
                  e.dma_start(out=wdb[:, f0 - 11 * half:f1 - 11 * half, :], in_=wdsc[n][:, f0:f1, :]),
                  reads=(("wdscr", n, 2 * half + q),), writes=(("wdb", q),), dma=("wdb", q))

    def load_x(i):
        buf = i % 2
        src = x_d[i * NT:(i + 1) * NT, :].rearrange("(c p) d -> p c d", p=128)
        S.add("sp", lambda e, buf=buf, src=src: e.dma_start(out=xb[buf], in_=src),
              writes=XALL(buf), dma=("xld", buf))

    def rsqrt_cols(src, dst, scale, rres, wres):
        S.add("dve", lambda e: e.tensor_scalar(out=dst, in0=src, scalar1=scale, scalar2=EPS,
                                               op0=ALU.mult, op1=ALU.add), reads=rres, writes=wres)
        S.add("act", lambda e: e.activation(out=dst, in_=dst, func=AF.Sqrt), reads=wres, writes=wres)
        S.add("dve", lambda e: e.reciprocal(out=dst, in_=dst), reads=wres, writes=wres)

    class Pipe:
        def __init__(self):
            self.items = []

        def at(self, step, fn):
            self.items.append((step, fn))

        def tick(self, step):
            for it in [it for it in self.items if it[0] <= step]:
                self.items.remove(it)
                it[1]()

        def flush(self):
            self.tick(10 ** 9)

    def stats_a(buf, c, col):
        xt = xb[buf]
        X = XR(buf, c)
        S.add("act", lambda e: e.activation(out=junk, in_=xt[:, c, :], func=AF.Square,
                                            accum_out=ssq[:, col:col + 1]),
              reads=(X,), writes=("junk", ("ssq", col)))
        S.add("act", lambda e: e.activation(out=rstd[:, col:col + 1], in_=ssq[:, col:col + 1], func=AF.Sqrt,
                                            scale=1.0 / D, bias=cst[:, 2:3]),
              reads=(("ssq", col), "cst"), writes=(("rstd", col),))

    def h_div(buf, c, col):
        xt = xb[buf]
        S.add("dve", lambda e: e.reciprocal(out=rstd[:, col:col + 1], in_=rstd[:, col:col + 1]),
              reads=(("rstd", col),), writes=(("rstd", col),))
        S.add("dve", lambda e: e.tensor_scalar(out=htok[c], in0=xt[:, c, :], scalar1=rstd[:, col:col + 1],
                                               scalar2=None, op0=ALU.mult),
              reads=(XR(buf, c), ("rstd", col)), writes=(("htok", c),))

    def h_T(c, gi, dst=None, dname="hT"):
        dst = hT if dst is None else dst
        hb = htok[c]
        HB = ("htok", c)
        pt = bankb[6 + c % 2]
        PT = PS(6 + c % 2)
        for k in range(8):
            S.add("pe", lambda e, k=k: e.transpose(out=pt[:, k * 128:(k + 1) * 128],
                                                   in_=hb[:, k * 128:(k + 1) * 128], identity=ident),
                  reads=(HB, "ident"), writes=(PT,))
        S.add("dve", lambda e: e.tensor_tensor(
            out=dst[:, :, c * 128:(c + 1) * 128], in0=pt.rearrange("p (k t) -> p k t", k=8),
            in1=gcol[:, gi, :].unsqueeze(2).to_broadcast([128, 8, 128]), op=ALU.mult),
            reads=(PT, ("gcol", gi)), writes=((dname, c),))

    def fin_scale(buf, c, col):
        xt = xb[buf]
        X = XR(buf, c)
        S.add("dve", lambda e: e.reciprocal(out=rstd[:, col:col + 1], in_=rstd[:, col:col + 1]),
              reads=(("rstd", col),), writes=(("rstd", col),))
        S.add("dve", lambda e: e.scalar_tensor_tensor(
            out=xt[:, c, :], in0=xt[:, c, :], scalar=rstd[:, col:col + 1], in1=gfin,
            op0=ALU.mult, op1=ALU.mult), reads=(X, ("rstd", col), "gfin"), writes=(X,))

    def norm_pipe(buf, gi, col0, lag=1, dst=None, dname="hT"):
        p = Pipe()
        for c in range(NCH):
            p.at(2 * c + lag, lambda c=c: stats_a(buf, c, col0 + c))
            p.at(2 * c + lag + 1, lambda c=c: h_div(buf, c, col0 + c))
            p.at(2 * c + lag + 2, lambda c=c: h_T(c, gi, dst, dname))
        return p

    def store_tile(i):
        buf = i % 2
        dst = out_d[i * NT:(i + 1) * NT, :].rearrange("(c p) d -> p c d", p=128)
        S.add("sp", lambda e: e.dma_start(out=dst, in_=xb[buf]),
              reads=XALL(buf), writes=(("outd", i),), dma=("xst", buf))

    def ffn_gu(i, n, half, src, srcres, overlap=False, hook=None):
        slot = None
        for fp in range(NFH):
            f0 = half * NFH + fp
            if f0 % 2 == 0 or fp == 0:
                slot = next_block(BLK_GU[n][f0 // 2])
                prefetch(3)
            if fp == 2:
                load_wd(n, half)
            if hook is not None and fp == 6:
                hook()
            f2 = f0 % 2
            RS = ("ring", slot)
            gi_ = f0 % 2
            ui_ = 2 if overlap else 2 + f0 % 2
            pg, pu = psb[gi_], psb[ui_]
            for k in range(8):
                S.add("pe", lambda e, k=k, slot=slot, f2=f2, pg=pg: e.matmul(
                    pg, lhsT=ring[slot][:, k * 256 + f2 * 128:k * 256 + f2 * 128 + 128],
                    rhs=src[:, k, :], start=(k == 0), stop=(k == 7)),
                    reads=(RS,) + srcres, writes=(PS(gi_),))
                if overlap and k == 3:
                    yield
            if overlap:
                yield
            for k in range(8):
                S.add("pe", lambda e, k=k, slot=slot, f2=f2, pu=pu: e.matmul(
                    pu, lhsT=ring[slot][:, 2048 + k * 256 + f2 * 128:2048 + k * 256 + f2 * 128 + 128],
                    rhs=src[:, k, :], start=(k == 0), stop=(k == 7)),
                    reads=(RS,) + srcres, writes=(PS(ui_),))
                if overlap and k == 3:
                    yield
            sg = sgs[f0 % 2]
            S.add("act", lambda e, pg=pg, sg=sg: e.activation(out=sg, in_=pg, func=AF.Silu),
                  reads=(PS(gi_),), writes=(("sgs", f0 % 2),))
            S.add("dve", lambda e, pu=pu, sg=sg, fp=fp: e.tensor_tensor(
                out=actb[:, fp, :], in0=sg, in1=pu, op=ALU.mult),
                reads=(PS(ui_), ("sgs", f0 % 2)), writes=(("actb", fp),))
            yield

    def ffn_down(i, n, half, pipe=None):
        buf = i % 2
        xt = xb[buf]
        for c in range(NCH):
            for dh in range(2):
                pi = 4 + (2 * c + dh) % 2
                po = psb[pi]
                for fp in range(NFH):
                    S.add("pe", lambda e, fp=fp, c=c, dh=dh, po=po: e.matmul(
                        po, lhsT=actb[:, fp, c * 128:(c + 1) * 128],
                        rhs=wdb[:, fp, dh * 512:(dh + 1) * 512],
                        start=(fp == 0), stop=(fp == NFH - 1)),
                        reads=(("actb", fp), ("wdb", 0 if fp < 6 else 1)), writes=(PS(pi),))
                S.add("dve", lambda e, c=c, dh=dh, po=po: e.scalar_tensor_tensor(
                    out=xt[:, c, dh * 512:(dh + 1) * 512], in0=po, scalar=0.5,
                    in1=xt[:, c, dh * 512:(dh + 1) * 512], op0=ALU.mult, op1=ALU.add),
                    reads=(PS(pi), XR(buf, c)), writes=(XR(buf, c),))
                if pipe is not None:
                    pipe.tick(2 * c + dh)
        if pipe is not None:
            pipe.flush()

    def run(gen):
        for _ in gen:
            pass

    def mixer(i, pipe=None, pre_a=None, pre_e=None, inter=None):
        buf = i % 2
        xt = xb[buf]

        def pull(k):
            if inter is not None:
                for _ in range(k):
                    next(inter, None)

        def gates():
            for c in range(NCH):
                for k in range(8):
                    S.add("pe", lambda e, c=c, k=k: e.matmul(
                        ps7[:, c * 8:(c + 1) * 8], lhsT=hT[:, k, c * 128:(c + 1) * 128], rhs=wif[:, k, :],
                        start=(k == 0), stop=(k == 7)), reads=(("hT", c), "wif"), writes=(PS(7),))
            S.add("dve", lambda e: e.tensor_tensor(
                out=gts, in0=ps7[:, 0:32].rearrange("p (c g) -> p c g", c=NCH),
                in1=gbias.unsqueeze(1).to_broadcast([128, NCH, 8]), op=ALU.add),
                reads=(PS(7), "gbi", "gbf"), writes=("gts",))
            S.add("act", lambda e: e.activation(out=eft, in_=gts[:, :, 4:8], func=AF.Exp, scale=-1.0),
                  reads=("gts",), writes=("eft",))
            S.add("act", lambda e: e.activation(out=lt, in_=eft, func=AF.Ln, bias=cst[:, 0:1]),
                  reads=("eft", "cst"), writes=("lt",))

        def gates_b():
            for c in range(NCH):
                S.add("pe", lambda e, c=c: e.matmul(ps7[:, 32 + c * 4:36 + c * 4], lhsT=tri, rhs=lt[:, c, :],
                                                    start=True, stop=True),
                      reads=("tri", "lt"), writes=(PS(7),))
                S.add("pe", lambda e, c=c: e.matmul(ps7[:, 48 + c * 2:50 + c * 2], lhsT=selA,
                                                    rhs=lt[:, c, 0:4:2], start=True, stop=False),
                      reads=("selA", "lt"), writes=(PS(7),))
                S.add("pe", lambda e, c=c: e.matmul(ps7[:, 48 + c * 2:50 + c * 2], lhsT=selB,
                                                    rhs=lt[:, c, 1:4:2], start=False, stop=True),
                      reads=("selB", "lt"), writes=(PS(7),))
            bpv = ps7[:, 32:48].rearrange("p (c h) -> p c h", c=NCH)
            S.add("dve", lambda e: e.tensor_tensor(out=t1t, in0=bpv, in1=gts[:, :, 0:4], op=ALU.add),
                  reads=(PS(7), "gts"), writes=("t1t",))
            S.add("act", lambda e: e.activation(out=ebt, in_=bpv, func=AF.Exp),
                  reads=(PS(7),), writes=("ebt",))
            S.add("act", lambda e: e.activation(out=dcn, in_=ps7[:, 48:56].rearrange("p (c j) -> p c j", c=NCH),
                                                func=AF.Exp, scale=-1.0),
                  reads=(PS(7),), writes=("dcn",))
            S.add("act", lambda e: e.activation(out=kap, in_=t1t, func=AF.Exp, bias=cst[:, 1:2]),
                  reads=("t1t", "cst"), writes=("kap",))

        def qk_proj(j, slot):
            RS = ("ring", slot)
            pz = psb[j % 2]
            for k in range(8):
                S.add("pe", lambda e, k=k: e.matmul(
                    pz, lhsT=ring[slot][:, k * 512 + j * 128:k * 512 + (j + 1) * 128], rhs=hT[:, k, :],
                    start=(k == 0), stop=(k == 7)), reads=(RS,) + HTALL, writes=(PS(j % 2),))
            ZQ = ("zq", j)
            S.add("act", lambda e: e.activation(out=zq[:, j, 3:NT + 3], in_=pz, func=AF.Copy),
                  reads=(PS(j % 2),), writes=(ZQ,))
            ca = cacc[j % 2]
            CA = ("cacc", j % 2)
            S.add("pool", lambda e: e.tensor_scalar(
                out=ca, in0=zq[:, j, 0:NT], scalar1=cw[:, j, 0:1], scalar2=cb[:, j:j + 1],
                op0=ALU.mult, op1=ALU.add), reads=(ZQ, ("cw", 0), "cb"), writes=(CA,))
            for tp in range(1, 4):
                S.add("dve", lambda e, tp=tp: e.scalar_tensor_tensor(
                    out=ca, in0=zq[:, j, tp:NT + tp], scalar=cw[:, j, tp:tp + 1], in1=ca,
                    op0=ALU.mult, op1=ALU.add), reads=(ZQ, ("cw", tp), CA), writes=(CA,))
            S.add("pool", lambda e: e.tensor_copy(out=zq[:, j, 0:3], in_=zq[:, j, NT:NT + 3]),
                  reads=(ZQ,), writes=(ZQ,))

        def qk_silu(j):
            ca = cacc[j % 2]
            CA = ("cacc", j % 2)
            if j < 2:
                S.add("act", lambda e: e.activation(out=qz[0:64, 2 * j, :], in_=ca[0:64, :], func=AF.Silu),
                      reads=(CA,), writes=(("qz", 2 * j),))
                S.add("act", lambda e: e.activation(out=qz[64:128, 2 * j + 1, :], in_=ca[64:128, :],
                                                    func=AF.Silu), reads=(CA,), writes=(("qz", 2 * j + 1),))
            else:
                S.add("act", lambda e: e.activation(out=qkT[:, j, :], in_=ca, func=AF.Silu),
                      reads=(CA,), writes=(("qkT", j),))

        def k_tr(c):
            for jj in range(2):
                S.add("pe", lambda e, jj=jj: e.transpose(
                    out=psT[:, jj * 128:(jj + 1) * 128], in_=qkT[:, 2 + jj, c * 128:(c + 1) * 128],
                    identity=ident), reads=(("qkT", 2 + jj), "ident"), writes=(PS(6),))
            S.add("act", lambda e: e.activation(out=ktok[:, c, :], in_=psT[:, 0:256], func=AF.Copy),
                  reads=(PS(6),), writes=(("ktok", c),))

        def vm_proj(c, slot):
            RS = ("ring", slot)
            pz = psb[2 + c]
            for k in range(8):
                S.add("pe", lambda e, k=k: e.matmul(
                    pz, lhsT=hT[:, k, c * 128:(c + 1) * 128], rhs=ring[slot][:, k * 512:(k + 1) * 512],
                    start=(k == 0), stop=(k == 7)), reads=(RS, ("hT", c)), writes=(PS(2 + c),))
            S.add("dve", lambda e: e.tensor_tensor(
                out=vaug[:, c, :, 0:128], in0=pz.rearrange("p (h v) -> p h v", h=4),
                in1=kap[:, c, :].unsqueeze(2).to_broadcast([128, 4, 128]), op=ALU.mult),
                reads=(PS(2 + c), "kap"), writes=(("vaug", c),))
            S.add("pool", lambda e: e.tensor_copy(out=vaug[:, c, :, 128:129], in_=kap[:, c, :].unsqueeze(2)),
                  reads=("kap",), writes=(("vaug", c),))

        def u_proj(g, slot):
            RS = ("ring", slot)
            pz = psb[g % 2]
            for k in range(8):
                S.add("pe", lambda e, k=k: e.matmul(
                    pz, lhsT=ring[slot][:, k * 512 + g * 128:k * 512 + (g + 1) * 128], rhs=hT[:, k, :],
                    start=(k == 0), stop=(k == 7)), reads=(RS,) + HTALL, writes=(PS(g % 2),))
            S.add("act", lambda e: e.activation(out=ug[:, g, :], in_=pz, func=AF.Gelu),
                  reads=(PS(g % 2),), writes=(("ug", g),))

        def vs_proj(c, slot):
            RS = ("ring", slot)
            pz = psb[2 + c % 2]
            for k in range(8):
                S.add("pe", lambda e, k=k: e.matmul(
                    pz, lhsT=hT[:, k, c * 128:(c + 1) * 128], rhs=ring[slot][:, k * 512:(k + 1) * 512],
                    start=(k == 0), stop=(k == 7)), reads=(RS, ("hT", c)), writes=(PS(2 + c % 2),))
            vg = vgs[c]
            VG = ("vgs", c)
            S.add("act", lambda e: e.activation(out=vg, in_=pz, func=AF.Gelu, accum_out=st1[:, c:c + 1]),
                  reads=(PS(2 + c % 2),), writes=(VG, ("st1", c)))
            S.add("act", lambda e: e.activation(out=junk[:, 0:NT], in_=vg, func=AF.Square,
                                                accum_out=st2[:, c:c + 1]),
                  reads=(VG,), writes=("junk", ("st2", c)))

        def ln_stats():
            ST = ("sgst",)
            st1r = tuple(("st1", c) for c in range(NCH))
            st2r = tuple(("st2", c) for c in range(NCH))
            S.add("dve", lambda e: e.tensor_scalar(out=mu_t, in0=st1[:, 0:4], scalar1=1.0 / 512, scalar2=None,
                                                   op0=ALU.mult), reads=st1r, writes=(ST,))
            S.add("dve", lambda e: e.tensor_tensor(out=var_t, in0=mu_t, in1=mu_t, op=ALU.mult),
                  reads=(ST,), writes=(ST,))
            S.add("dve", lambda e: e.scalar_tensor_tensor(out=var_t, in0=st2[:, 0:4], scalar=1.0 / 512, in1=var_t,
                                                          op0=ALU.mult, op1=ALU.subtract),
                  reads=(ST,) + st2r, writes=(ST,))
            S.add("act", lambda e: e.activation(out=rs_t, in_=var_t, func=AF.Sqrt, bias=cst[:, 2:3]),
                  reads=(ST, "cst"), writes=(ST,))
            S.add("dve", lambda e: e.reciprocal(out=rs_t, in_=rs_t), reads=(ST,), writes=(ST,))
            S.add("dve", lambda e: e.scalar_tensor_tensor(out=nmr_t, in0=mu_t, scalar=-1.0, in1=rs_t,
                                                          op0=ALU.mult, op1=ALU.mult), reads=(ST,), writes=(ST,))

        def sgu_chunk(c):
            ST = ("sgst",)
            vg = vgs[c]
            VG = ("vgs", c)
            vh = vhat[c % 2]
            VH = ("vhat", c % 2)
            S.add("pool", lambda e: e.tensor_scalar(
                out=vh, in0=vg, scalar1=rs_t[:, c:c + 1], scalar2=nmr_t[:, c:c + 1],
                op0=ALU.mult, op1=ALU.add), reads=(VG, ST), writes=(VH,))
            pull(1)
            for g in range(4):
                S.add("pe", lambda e, g=g: e.matmul(
                    bank[6][:, g * 128:(g + 1) * 128], lhsT=vh[:, g * 128:(g + 1) * 128], rhs=WmT[:, g, :],
                    start=True, stop=True), reads=(VH, "WmT"), writes=(PS(6),))
            S.add("dve", lambda e: e.tensor_tensor(
                out=tmpA, in0=bank[6].rearrange("p (g t) -> p g t", g=4), in1=lngB, op=ALU.mult),
                reads=(PS(6), "lngB"), writes=("tmpA",))
            S.add("pool", lambda e: e.tensor_tensor(out=tmpB, in0=tmpA, in1=T2, op=ALU.add),
                  reads=("tmpA", "T2"), writes=("tmpB",))
            S.add("pool", lambda e: e.tensor_tensor(
                out=yT[:, 0:4, c * 128:(c + 1) * 128], in0=tmpB, in1=ug[:, :, c * 128:(c + 1) * 128],
                op=ALU.mult), reads=("tmpB",) + tuple(("ug", g) for g in range(4)), writes=(("yTa", c),))

        def o_proj(jo, slot):
            RS = ("ring", slot)
            pz = psb[jo % 2]
            for k in range(8):
                S.add("pe", lambda e, k=k: e.matmul(
                    pz, lhsT=ring[slot][:, k * 512 + jo * 128:k * 512 + (jo + 1) * 128], rhs=hT[:, k, :],
                    start=(k == 0), stop=(k == 7)), reads=(RS,) + HTALL, writes=(PS(jo % 2),))
            S.add("act", lambda e: e.activation(out=sigo[:, jo, :], in_=pz, func=AF.Sigmoid),
                  reads=(PS(jo % 2),), writes=(("sigo", jo),))

        QZ = tuple(("qz", h) for h in range(4))

        def e_a(c):
            tcs = slice(c * 128, (c + 1) * 128)
            pull(1)
            for h in range(4):
                S.add("pe", lambda e, h=h: e.matmul(
                    psb[4][:, h * 128:(h + 1) * 128], lhsT=qkT[:, 2 + h // 2, tcs],
                    rhs=qz[:, h, tcs], start=True, stop=True),
                    reads=(("qkT", 2 + h // 2), ("qz", h)), writes=(PS(4),))
            wt = wTt[c % 2]
            WT = ("wTt", c % 2)
            S.add("dve", lambda e: e.tensor_tensor(
                out=wt, in0=psb[4].rearrange("p (h t) -> p h t", h=4),
                in1=tri.unsqueeze(1).to_broadcast([128, 4, 128]), op=ALU.mult),
                reads=(PS(4), "tri"), writes=(WT,))
            SB = ("Sbf", c % 2)
            S.add("pool", lambda e: e.tensor_tensor(
                out=Sbf[c % 2][:, :, 0:129], in0=Rst[:, :, 0:129],
                in1=dcol.unsqueeze(2).to_broadcast([128, 2, 129]), op=ALU.mult),
                reads=("Rst", "dcol"), writes=(SB,))
            for j in range(2):
                pull(1)
                for a in range(2):
                    h = 2 * j + a
                    S.add("pe", lambda e, h=h, j=j, a=a: e.matmul(
                        bank[5 + 2 * j][:, a * 130:a * 130 + 129], lhsT=ktok[:, c, j * 128:(j + 1) * 128],
                        rhs=vaug[:, c, h, 0:129], start=True, stop=True),
                        reads=(("ktok", c), ("vaug", c)), writes=(PS(5 + 2 * j),))
                for a in range(2):
                    p0 = a * 64
                    S.add("dve", lambda e, j=j, a=a, p0=p0: e.scalar_tensor_tensor(
                        out=Rst[p0:p0 + 64, j, 0:129], in0=Rst[p0:p0 + 64, j, 0:129],
                        scalar=dcol[p0:p0 + 64, j:j + 1],
                        in1=bank[5 + 2 * j][p0:p0 + 64, a * 130:a * 130 + 129],
                        op0=ALU.mult, op1=ALU.add),
                        reads=("Rst", "dcol", PS(5 + 2 * j)), writes=("Rst",))
            S.add("pool", lambda e: e.tensor_copy(out=dcol, in_=dcn[:, c, :]),
                  reads=("dcn",), writes=("dcol",))

        def e_b(c):
            tcs = slice(c * 128, (c + 1) * 128)
            wt = wTt[c % 2]
            WT = ("wTt", c % 2)
            SB = ("Sbf", c % 2)
            for j in range(2):
                pull(1)
                for a in range(2):
                    h = 2 * j + a
                    po = psb[3 + j][:, a * 130:a * 130 + 129]
                    S.add("pe", lambda e, h=h, po=po: e.matmul(
                        po, lhsT=wt[:, h, :], rhs=vaug[:, c, h, 0:129], start=True, stop=False),
                        reads=(WT, ("vaug", c)), writes=(PS(3 + j),))
                    S.add("pe", lambda e, h=h, j=j, po=po: e.matmul(
                        po, lhsT=qz[:, h, tcs], rhs=Sbf[c % 2][:, j, 0:129], start=False, stop=True),
                        reads=(("qz", h), SB), writes=(PS(3 + j),))
                S.add("act", lambda e, j=j: e.activation(out=numsb[:, j, :], in_=psb[3 + j][:, 0:260], func=AF.Copy),
                      reads=(PS(3 + j),), writes=(("numsb", j),))
            for j in range(2):
                pn = numsb[:, j, :].rearrange("p (a v) -> p a v", a=2)
                S.add("dve", lambda e, j=j, pn=pn: e.tensor_tensor(
                    out=dabs[:, 2 * j:2 * j + 2].unsqueeze(2), in0=pn[:, :, 128:129],
                    in1=ebt[:, c, 2 * j:2 * j + 2].unsqueeze(2), op=ALU.max),
                    reads=(("numsb", j), "ebt"), writes=(("dabs", j),))
                S.add("dve", lambda e, j=j, pn=pn: e.scalar_tensor_tensor(
                    out=dabs[:, 2 * j:2 * j + 2].unsqueeze(2), in0=pn[:, :, 128:129], scalar=-1.0,
                    in1=dabs[:, 2 * j:2 * j + 2].unsqueeze(2), op0=ALU.mult, op1=ALU.max),
                    reads=(("numsb", j), ("dabs", j)), writes=(("dabs", j),))
                for a in range(2):
                    h = 2 * j + a
                    S.add("act", lambda e, h=h, a=a, pn=pn: e.activation(
                        out=junk[:, 0:128], in_=pn[:, a, 0:128], func=AF.Square,
                        accum_out=ssqn[:, h:h + 1]), reads=(("numsb", j),), writes=("junk", ("ssqn", h)))
            HS = ("hstat",)
            S.add("dve", lambda e: e.reciprocal(out=rcp, in_=dabs),
                  reads=(("dabs", 0), ("dabs", 1)), writes=(HS,))
            S.add("dve", lambda e: e.tensor_tensor(out=sct, in0=rcp, in1=rcp, op=ALU.mult),
                  reads=(HS,), writes=(HS,))
            S.add("dve", lambda e: e.tensor_tensor(out=sct, in0=sct, in1=ssqn, op=ALU.mult),
                  reads=(HS,) + tuple(("ssqn", h) for h in range(4)), writes=(HS,))
            S.add("act", lambda e: e.activation(out=sct, in_=sct, func=AF.Sqrt, scale=1.0 / 128, bias=cst[:, 2:3]),
                  reads=(HS, "cst"), writes=(HS,))
            S.add("dve", lambda e: e.reciprocal(out=sct, in_=sct), reads=(HS,), writes=(HS,))
            S.add("dve", lambda e: e.tensor_tensor(out=sc2, in0=sct, in1=rcp, op=ALU.mult),
                  reads=(HS,), writes=(("sc2",),))
            hb = hn[c % 2]
            HN = ("hn", c % 2)
            for j in range(2):
                pn = numsb[:, j, :].rearrange("p (a v) -> p a v", a=2)
                S.add("dve", lambda e, j=j, pn=pn: e.tensor_tensor(
                    out=hb[:, 2 * j:2 * j + 2, :], in0=pn[:, :, 0:128],
                    in1=sc2[:, 2 * j:2 * j + 2].unsqueeze(2).to_broadcast([128, 2, 128]), op=ALU.mult),
                    reads=(("numsb", j), ("sc2",)), writes=(HN,))

        def e_c(c):
            tcs = slice(c * 128, (c + 1) * 128)
            hb = hn[c % 2]
            HN = ("hn", c % 2)
            pull(1)
            for h in range(4):
                S.add("pe", lambda e, h=h: e.transpose(
                    out=psT[:, h * 128:(h + 1) * 128], in_=hb[:, h, :], identity=ident),
                    reads=(HN, "ident"), writes=(PS(6),))
            for h in range(4):
                S.add("dve", lambda e, h=h: e.scalar_tensor_tensor(
                    out=yT[:, 4 + h, tcs], in0=psT[:, h * 128:(h + 1) * 128], scalar=mhnc[:, h:h + 1],
                    in1=sigo[:, h, tcs], op0=ALU.mult, op1=ALU.mult),
                    reads=(PS(6), "mhnc", ("sigo", h)), writes=(("yTb", c, h),))

        if pre_a is not None:
            pre_a()
        gates()
        slot = next_block(BLK_QK)
        prefetch(3)
        for j in range(4):
            qk_proj(j, slot)
            if j == 1:
                gates_b()
            if j >= 1:
                qk_silu(j - 1)
            drain_conv(1)
        qk_silu(3)
        slot = next_block(BLK_VM)
        prefetch(3)
        for c in range(NCH):
            vm_proj(c, slot)
            drain_conv(1)
        slot = next_block(BLK_U)
        prefetch(3)
        for g in range(4):
            u_proj(g, slot)
            drain_conv(1)
        slot = next_block(BLK_VS)
        prefetch(3)
        for c in range(NCH):
            vs_proj(c, slot)
            drain_conv(1)
        ln_stats()
        for c in range(NCH):
            k_tr(c)
        slot = next_block(BLK_O)
        prefetch(3)
        for jo in range(4):
            o_proj(jo, slot)
            drain_conv(1)
        if pre_e is not None:
            pre_e()

        for s_ in range(NCH + 2):
            if s_ < NCH:
                e_a(s_)
            pull(1)
            if 0 <= s_ - 1 < NCH:
                e_b(s_ - 1)
            if s_ < NCH:
                sgu_chunk(s_)
            pull(1)
            if 0 <= s_ - 2 < NCH:
                e_c(s_ - 2)
            drain_conv(1)
        pull(99)
        drain_conv(99)

        slots = (next_block(BLK_WO0), next_block(BLK_WO1))
        prefetch(2)
        for c in range(NCH):
            for dh in range(2):
                slot = slots[dh]
                RS = ("ring", slot)
                pi = (2 * c + dh) % 2
                po = psb[pi]
                for m in range(8):
                    S.add("pe", lambda e, m=m, c=c, slot=slot, po=po: e.matmul(
                        po, lhsT=yT[:, m, c * 128:(c + 1) * 128], rhs=ring[slot][:, m * 512:(m + 1) * 512],
                        start=(m == 0), stop=(m == 7)),
                        reads=(RS, ("yTa", c)) + tuple(("yTb", c, hh) for hh in range(4)), writes=(PS(pi),))
                S.add("dve", lambda e, c=c, dh=dh, po=po: e.tensor_tensor(
                    out=xt[:, c, dh * 512:(dh + 1) * 512], in0=po, in1=xt[:, c, dh * 512:(dh + 1) * 512],
                    op=ALU.add), reads=(PS(pi), XR(buf, c)), writes=(XR(buf, c),))
                if pipe is not None:
                    pipe.tick(2 * c + dh)

    norm_pipe(0, 0, 0, dst=hTf, dname="hTf").flush()
    prefetch(2)
    run(ffn_gu(0, 0, 0, hTf, HTFALL))
    S.add("sp", lambda e: e.dma_start(out=wif, in_=wifsc), reads=(("wifscr",),), writes=("wif",),
          dma=("cwif",))
    ffn_down(0, 0, 0)
    for i in range(ntiles):
        buf = i % 2
        more = i + 1 < ntiles
        nxt = (lambda i=i: load_x(i + 1)) if more else None
        run(ffn_gu(i, 0, 1, hTf, HTFALL, hook=nxt))
        pn = norm_pipe(buf, 1, 4)
        if more:
            for c in range(NCH):
                pn.at(2 * c + 1, lambda c=c, buf=buf: stats_a(1 - buf, c, c))
                pn.at(2 * c + 4, lambda c=c, buf=buf: h_div(1 - buf, c, c))
        ffn_down(i, 0, 1, pipe=pn)
        inter = ffn_gu(i + 1, 0, 0, hTf, HTFALL, overlap=True) if more else None
        def pre_a(buf=buf):
            for c in range(NCH):
                stats_a(1 - buf, c, c)
            for c in range(NCH):
                h_div(1 - buf, c, c)

        def pre_b():
            for c in range(NCH):
                h_T(c, 0, hTf, "hTf")
        pm = norm_pipe(buf, 2, 8)
        mixer(i, pipe=pm, pre_a=None, pre_e=pre_b if more else None, inter=inter)
        if more:
            rest = Pipe()
            for n_, it in enumerate(sorted(pm.items, key=lambda it: it[0])):
                rest.at(n_, it[1])
            ffn_down(i + 1, 0, 0, pipe=rest)
        else:
            pm.flush()
        p2 = Pipe()
        for c in range(NCH):
            p2.at(2 * c + 1, lambda c=c, buf=buf: stats_a(buf, c, 12 + c))
            p2.at(2 * c + 2, lambda c=c, buf=buf: fin_scale(buf, c, 12 + c))
        run(ffn_gu(i, 1, 0, hT, HTALL))
        ffn_down(i, 1, 0)
        run(ffn_gu(i, 1, 1, hT, HTALL))
        ffn_down(i, 1, 1, pipe=p2)
        store_tile(i)
    finals = [("xst", 0)] + ([("xst", 1)] if ntiles > 1 else [])
    if debug:
        dumps = {
            "yT": (yT, [("yTa", c) for c in range(4)] + [("yTb", c, h) for c in range(4) for h in range(4)]),
            "qz": (qz, [("qz", h) for h in range(4)]), "qkT": (qkT, [("qkT", 2), ("qkT", 3)]), "ug": (ug, [("ug", g) for g in range(4)]),
            "sigo": (sigo, [("sigo", g) for g in range(4)]), "vaug": (vaug, [("vaug", c) for c in range(4)]),
            "kap": (kap, ["kap"]), "ebt": (ebt, ["ebt"]), "dcn": (dcn, ["dcn"]), "gts": (gts, ["gts"]),
            "lt": (lt, ["lt"]), "ktok": (ktok, [("ktok", c) for c in range(4)]), "T2": (T2, ["T2"]),
            "WmT": (WmT, ["WmT"]), "Rst": (Rst, ["Rst"]), "hn1": (hn[1], [("hn", 1)]), "wT1": (wTt[1], [("wTt", 1)]),
            "zq": (zq, [("zq", j) for j in range(4)]), "vhat1": (vhat[1], [("vhat", 1)]),
            "tri": (tri, ["tri"]), "cw": (cw, [("cw", k) for k in range(4)]), "gcol": (gcol, [("gcol", k) for k in range(3)]),
            "hT": (hT, list(HTALL)), "st1": (st1, [("sgst",)]), "st2": (st2, [("sgst",)]), "mu_t": (mu_t, [("sgst",)]),
            "var_t": (var_t, [("sgst",)]), "rs_t": (rs_t, [("sgst",)]), "nmr_t": (nmr_t, [("sgst",)]), "eft": (eft, ["eft"]),
            "vgs3": (vgs[3], [("vgs", 3)]), "sc2": (sc2, [("sc2",)]), "rcp": (rcp, [("hstat",)]), "Sbf": (Sbf[0], [("Sbf", 0)]),
        }
        for name, (ap, rres) in dumps.items():
            dd = nc.dram_tensor("dbg_" + name, list(ap.shape), ap.dtype, kind="ExternalOutput").ap()
            S.add("sp", lambda e, dd=dd, ap=ap: e.dma_start(out=dd, in_=ap), reads=tuple(rres),
                  writes=(("dbgd", name),), dma=("dbg",), total=True)
        finals.append(("dbg",))
    S.emit(nc, finals)
    return nc


_DBG = dict(stop=9, skip=0, mstop=99, estop=99, debug=False, sink=None, pad=0)

_W_NAMES = ("ffn1_norm", "ffn1_w_gate", "ffn1_w_up", "ffn1_w_down", "mix_norm", "w_in", "sgu_ln_g",
            "sgu_ln_b", "sgu_w", "sgu_b", "conv_w", "conv_b", "igate_b", "fgate_b", "mh_norm", "w_out",
            "ffn2_norm", "ffn2_w_gate", "ffn2_w_up", "ffn2_w_down")


def kernel(**inputs):
    x = np.ascontiguousarray(np.asarray(inputs["x"], dtype=np.float32))
    B, T, _ = x.shape
    shared = {}
    for n in _W_NAMES:
        a = np.asarray(inputs[n], dtype=np.float32)
        shared[n] = np.ascontiguousarray(a[0])
    shared["final_norm"] = np.ascontiguousarray(np.asarray(inputs["final_norm"], dtype=np.float32))
    nc = build_program(T, _DBG['stop'], _DBG['skip'], _DBG['mstop'], _DBG['estop'], _DBG['debug'])
    in_maps = [dict(shared, x=x[b]) for b in range(B)]
    res = run_bass_kernel_spmd(nc, in_maps, core_ids=list(range(B)))
    if _DBG['sink'] is not None:
        _DBG['sink'](res.results)
    return np.stack([np.asarray(r["out"], dtype=np.float32) for r in res.results], axis=0)
```

```python
import math
import numpy as np
import concourse.bass as bass
import concourse.mybir as mybir
from concourse.bass_utils import run_bass_kernel_spmd

F32 = mybir.dt.float32
BF16 = mybir.dt.bfloat16
AF = mybir.ActivationFunctionType
ALU = mybir.AluOpType

D = 1024
DFF = 2816
NF = 22
NFH = 11
INW = 2568
NT = 512
NCH = 4
EPS = 1e-6
VS = 144
LN8 = math.log(8.0)


class _Op:
    __slots__ = ("eng", "fn", "deps", "signal", "sigval", "dma_key", "dma_val", "is_dma", "dma_total", "idx")


class Sched:
    ENGS = ("pe", "act", "dve", "pool", "sp")

    def __init__(self):
        self.ops = []
        self.last_w = {}
        self.readers = {}
        self.dma_counts = {}

    def add(self, eng, fn, reads=(), writes=(), dma=None, total=False):
        op = _Op()
        op.dma_total = total
        op.eng = eng
        op.fn = fn
        op.signal = False
        op.sigval = 0
        op.is_dma = dma is not None
        op.dma_key = dma
        op.dma_val = 0
        deps = []
        psr = tuple(r for r in reads if isinstance(r, tuple) and r[0] == "ps" and r not in writes)
        if psr:
            writes = tuple(writes) + psr
        raw_ids = set()
        for r in reads:
            w = self.last_w.get(r)
            if w is not None:
                deps.append(w)
                raw_ids.add(id(w))
        for w_ in writes:
            w = self.last_w.get(w_)
            if w is not None:
                deps.append(w)
            deps.extend(self.readers.get(w_, ()))
        for r in reads:
            self.readers.setdefault(r, []).append(op)
        for w_ in writes:
            self.last_w[w_] = op
            self.readers[w_] = []
        op.idx = len(self.ops)
        best = {}
        for d in deps:
            if d is op:
                continue
            same_eng = (d.eng == eng and eng in ("act", "dve", "pool"))
            if not (d.is_dma or d.eng != eng or same_eng):
                continue
            key = ("d", d.dma_key) if d.is_dma else ("e", d.eng)
            cur = best.get(key)
            if cur is None or d.idx > cur.idx:
                best[key] = d
        op.deps = list(best.values())
        for d in op.deps:
            if not d.is_dma:
                d.signal = True
        if op.is_dma:
            self.dma_counts[dma] = self.dma_counts.get(dma, 0) + 1
            op.dma_val = 16 * self.dma_counts[dma]
        self.ops.append(op)
        return op

    def emit(self, nc, final_waits):
        counts = {e: 0 for e in self.ENGS}
        for op in self.ops:
            if op.signal and not op.is_dma:
                counts[op.eng] += 1
                op.sigval = counts[op.eng]
        per_eng = {e: [op for op in self.ops if op.eng == e] for e in self.ENGS}
        import contextlib
        with contextlib.ExitStack() as st:
            esem = {e: st.enter_context(nc.semaphore("s_" + e)) for e in self.ENGS}
            dsem = {k: st.enter_context(nc.semaphore("d_%d" % i))
                    for i, k in enumerate(self.dma_counts)}
            block = st.enter_context(nc.Block())

            def run(engname, handle):
                waited = {}
                for op in per_eng[engname]:
                    for d in op.deps:
                        if d.is_dma:
                            val = 16 * self.dma_counts[d.dma_key] if d.dma_total else d.dma_val
                            sem, key = dsem[d.dma_key], ("d", d.dma_key)
                        else:
                            sem, val, key = esem[d.eng], d.sigval, ("e", d.eng)
                        if waited.get(key, 0) < val:
                            handle.wait_ge(sem, val)
                            waited[key] = val
                    ins = op.fn(handle)
                    if op.is_dma:
                        ins.then_inc(dsem[op.dma_key], 16)
                    elif op.signal:
                        ins.then_inc(esem[op.eng], 1)
                if engname == "sp":
                    for k in final_waits:
                        handle.wait_ge(dsem[k], 16 * self.dma_counts[k])

            @block.tensor
            def _(e):
                run("pe", e)

            @block.scalar
            def _(e):
                run("act", e)

            @block.vector
            def _(e):
                run("dve", e)

            @block.gpsimd
            def _(e):
                run("pool", e)

            @block.sync
            def _(e):
                run("sp", e)


def build_program(T, stop_after=9, skip=0, mstop=99, estop=99, debug=False):
    assert T % NT == 0
    ntiles = T // NT
    nc = bass.Bass("TRN2", target_bir_lowering=False)
    S = Sched()

    def din(name, shape):
        return nc.dram_tensor(name, list(shape), F32, kind="ExternalInput").ap()

    x_d = din("x", [T, D])
    norm_d = {n: din(n, [D]) for n in ("ffn1_norm", "mix_norm", "ffn2_norm", "final_norm")}
    wg_d = [din("ffn1_w_gate", [D, DFF]), din("ffn2_w_gate", [D, DFF])]
    wu_d = [din("ffn1_w_up", [D, DFF]), din("ffn2_w_up", [D, DFF])]
    wd_d = [din("ffn1_w_down", [DFF, D]), din("ffn2_w_down", [DFF, D])]
    win_d = din("w_in", [D, INW])
    lng_d = din("sgu_ln_g", [512])
    lnb_d = din("sgu_ln_b", [512])
    sguw_d = din("sgu_w", [4, 128, 128])
    sgub_d = din("sgu_b", [4, 128])
    convw_d = din("conv_w", [4, 512])
    convb_d = din("conv_b", [512])
    igb_d = din("igate_b", [4])
    fgb_d = din("fgate_b", [4])
    mhn_d = din("mh_norm", [512])
    wout_d = din("w_out", [D, D])
    out_d = nc.dram_tensor("out", [T, D], F32, kind="ExternalOutput").ap()

    NBLK = 29
    if _DBG.get("pad"):
        nc.dram_tensor("padscr", [_DBG["pad"], 128, 4096], BF16, kind="Internal")
    wsc = nc.dram_tensor("wsc", [NBLK, 128, 4096], BF16, kind="Internal").ap()
    wdsc = [nc.dram_tensor("wdsc%d" % n, [128, NF, D], BF16, kind="Internal").ap() for n in range(2)]
    wifsc = nc.dram_tensor("wifsc", [128, 8, 8], BF16, kind="Internal").ap()
    BLK_GU = [list(range(0, 11)), list(range(18, 29))]
    BLK_U, BLK_VS, BLK_QK, BLK_VM, BLK_O, BLK_WO0, BLK_WO1 = 11, 12, 13, 14, 15, 16, 17

    def sb(name, shape, dt=F32):
        return nc.alloc_sbuf_tensor(name, list(shape), dt).ap()

    xb = [sb("xb%d" % i, [128, NCH, D]) for i in range(2)]
    htok = [sb("htok%d" % i, [128, D], BF16) for i in range(4)]
    hT = sb("hT", [128, 8, NT], BF16)
    hTf = sb("hTf", [128, 8, NT], BF16)
    numsb = sb("numsb", [128, 2, 260])
    actb = sb("actb", [128, NFH, NT], BF16)
    sgs = [sb("sgs%d" % i, [128, NT]) for i in range(2)]
    NRING = 4
    ring = [sb("ring%d" % i, [128, 4096], BF16) for i in range(NRING)]
    wdb = sb("wdb", [128, NFH, D], BF16)
    gfin = sb("gfin", [128, D])
    junk = sb("junk", [128, D], BF16)
    ssq = sb("ssq", [128, 16])
    rstd = sb("rstd", [128, 16])
    gcol = sb("gcol", [128, 3, 8])
    zq = sb("zq", [128, 4, NT + 3])
    cacc = [sb("cacc%d" % i, [128, NT]) for i in range(2)]
    qkT = sb("qkT", [128, 4, NT], BF16)
    ktok = sb("ktok", [128, NCH, 256], BF16)
    qz = sb("qz", [128, 4, NT], BF16)
    vaug = sb("vaug", [128, NCH, 4, VS], BF16)
    ug = sb("ug", [128, 4, NT], BF16)
    vgs = [sb("vgs%d" % i, [128, NT]) for i in range(4)]
    vhat = [sb("vhat%d" % i, [128, NT], BF16) for i in range(2)]
    sigo = sb("sigo", [128, 4, NT], BF16)
    yT = sb("yT", [128, 8, NT], BF16)
    wTt = [sb("wTt%d" % i, [128, 4, 128], BF16) for i in range(2)]
    hn = [sb("hn%d" % i, [128, 4, 128], BF16) for i in range(2)]
    tmpA = sb("tmpA", [128, 4, 128])
    tmpB = sb("tmpB", [128, 4, 128])
    Rst = sb("Rst", [128, 2, VS])
    Sbf = [sb("Sbf%d" % i, [128, 2, VS], BF16) for i in range(2)]
    dcol = sb("dcol", [128, 2])
    ident = sb("ident", [128, 128], BF16)
    identf = sb("identf", [128, 128])
    tri = sb("tri", [128, 128])
    selA = sb("selA", [128, 128])
    selB = sb("selB", [128, 128])
    onesf = sb("onesf", [128, 128])
    WmT = sb("WmT", [128, 4, 128], BF16)
    T2 = sb("T2", [128, 4, 128])
    lngB = sb("lngB", [128, 4, 128])
    lngc = sb("lngc", [128, 4])
    lnbc = sb("lnbc", [128, 4])
    mhnc = sb("mhnc", [128, 4])
    cw = sb("cw", [128, 4, 4])
    cb = sb("cb", [128, 4])
    gbias = sb("gbias", [128, 8])
    wif = sb("wif", [128, 8, 8], BF16)
    gts = sb("gts", [128, NCH, 8])
    eft = sb("eft", [128, NCH, 4])
    lt = sb("lt", [128, NCH, 4])
    t1t = sb("t1t", [128, NCH, 4])
    kap = sb("kap", [128, NCH, 4])
    ebt = sb("ebt", [128, NCH, 4])
    dcn = sb("dcn", [128, NCH, 2])
    st1 = sb("st1", [128, 8])
    st2 = sb("st2", [128, 8])
    mu_t = sb("mu_t", [128, 4])
    var_t = sb("var_t", [128, 4])
    rs_t = sb("rs_t", [128, 4])
    nmr_t = sb("nmr_t", [128, 4])
    dabs = sb("dabs", [128, 4])
    rcp = sb("rcp", [128, 4])
    ssqn = sb("ssqn", [128, 4])
    sct = sb("sct", [128, 4])
    sc2 = sb("sc2", [128, 4])
    cst = sb("cst", [128, 4])

    bank = [nc.alloc_psum_tensor("bank%d" % i, [128, 512], F32).ap() for i in range(8)]
    bankb = [b_.bitcast(BF16) for b_ in bank]
    psb = bank[0:6]
    psT = bankb[6]
    ps7 = bank[7]
    PS = lambda i: ("ps", i)
    XR = lambda buf, c: ("xb", buf, c)
    XALL = lambda buf: tuple(("xb", buf, c) for c in range(NCH))
    HTALL = tuple(("hT", c) for c in range(NCH))
    HTFALL = tuple(("hTf", c) for c in range(NCH))

    def conv_dma(dst, src, wres, key):
        if skip & 1:
            return
        S.add("pool", lambda e, dst=dst, src=src: e.dma_start(out=dst, in_=src),
              reads=(), writes=(wres,), dma=key, total=True)

    def kview(w):
        return w.rearrange("(kc p) n -> p kc n", p=128)

    def conv_gu(n):
        for fp in range(NFH):
            b = BLK_GU[n][fp]
            conv_dma(wsc[b][:, 0:2048].rearrange("p (k m) -> p k m", k=8),
                     kview(wg_d[n])[:, :, fp * 256:(fp + 1) * 256], ("scr", b, 0), ("cvgu", n, fp if n == 0 else 0))
            conv_dma(wsc[b][:, 2048:4096].rearrange("p (k m) -> p k m", k=8),
                     kview(wu_d[n])[:, :, fp * 256:(fp + 1) * 256], ("scr", b, 1), ("cvgu", n, fp if n == 0 else 0))

    def conv_wd(n):
        wv = wd_d[n].rearrange("(f p) n -> p f n", p=128)
        for q, (f0, f1) in enumerate(((0, 6), (6, 11), (11, 17), (17, 22))):
            conv_dma(wdsc[n][:, f0:f1, :], wv[:, f0:f1, :], ("wdscr", n, q), ("cvd", n, q if n == 0 else 0))

    def conv_in(b, c0):
        conv_dma(wsc[b].rearrange("p (k m) -> p k m", k=8), kview(win_d)[:, :, c0:c0 + 512],
                 ("scr", b, 0), ("cvmix", b))

    S.add("sp", lambda e: e.dma_start(out=xb[0], in_=x_d[0:NT, :].rearrange("(c p) d -> p c d", p=128)),
          writes=XALL(0), dma=("xld", 0))

    def ld_const(dst, src, res, slow=False):
        if skip & 2:
            return
        S.add("sp", lambda e, dst=dst, src=src, slow=slow:
              e.dma_start(out=dst, in_=src, allow_slow_non_contiguous=slow),
              reads=(), writes=(res,), dma=("consts",), total=True)

    ld_const(gfin, norm_d["final_norm"].partition_broadcast(128), "gfin")
    for i, n in enumerate(("ffn1_norm", "mix_norm", "ffn2_norm")):
        ld_const(gcol[:, i, :], norm_d[n].rearrange("(k p) -> p k", p=128), ("gcol", i), slow=True)
    ld_const(lngc, lng_d.rearrange("(g p) -> p g", p=128), "lngc", slow=True)
    ld_const(lnbc, lnb_d.rearrange("(g p) -> p g", p=128), "lnbc", slow=True)
    ld_const(mhnc, mhn_d.rearrange("(g p) -> p g", p=128), "mhnc", slow=True)
    ld_const(cb, convb_d.rearrange("(j p) -> p j", p=128), "cb", slow=True)
    for k in range(4):
        ld_const(cw[:, :, k], convw_d[k].rearrange("(j p) -> p j", p=128), ("cw", k), slow=True)
    ld_const(gbias[:, 0:4], igb_d.partition_broadcast(128), "gbi")
    ld_const(gbias[:, 4:8], fgb_d.partition_broadcast(128), "gbf")
    wsg32 = xb[1][:, 0, 0:512].rearrange("p (g s) -> p g s", g=4)
    WmT32 = xb[1][:, 0, 512:1024].rearrange("p (g s) -> p g s", g=4)
    biasB = xb[1][:, 1, 0:512].rearrange("p (g s) -> p g s", g=4)
    XB1 = ("xb", 1, 0)
    XB1b = ("xb", 1, 1)
    S.add("sp", lambda e: e.dma_start(out=wsg32, in_=sguw_d.rearrange("g t s -> t g s")),
          writes=("wsg32",), dma=("consts",), total=True)
    S.add("sp", lambda e: e.dma_start(out=biasB, in_=sgub_d.partition_broadcast(128)),
          writes=("biasB",), dma=("consts",), total=True)

    S.add("pool", lambda e: e.memset(ident, 0.0), writes=("ident",))
    S.add("pool", lambda e: e.affine_select(out=ident, in_=ident, pattern=[[-1, 128]],
                                            compare_op=ALU.not_equal, fill=1.0, base=0,
                                            channel_multiplier=1), reads=("ident",), writes=("ident",))
    S.add("pool", lambda e: e.memset(identf, 0.0), writes=("identf",))
    S.add("pool", lambda e: e.affine_select(out=identf, in_=identf, pattern=[[-1, 128]],
                                            compare_op=ALU.not_equal, fill=1.0, base=0,
                                            channel_multiplier=1), reads=("identf",), writes=("identf",))
    S.add("pool", lambda e: e.memset(onesf, 1.0), writes=("onesf",))
    S.add("pool", lambda e: e.memset(cst[:, 0:1], 1.0), writes=("cst",))
    S.add("pool", lambda e: e.memset(cst[:, 1:2], -LN8), writes=("cst",))
    S.add("pool", lambda e: e.memset(cst[:, 2:3], EPS), writes=("cst",))
    S.add("pool", lambda e: e.memset(tri, 1.0), writes=("tri",))
    S.add("pool", lambda e: e.affine_select(out=tri, in_=tri, pattern=[[1, 128]],
                                            compare_op=ALU.is_ge, fill=0.0, base=0,
                                            channel_multiplier=-1), reads=("tri",), writes=("tri",))
    S.add("pool", lambda e: e.memset(selA, 0.0), writes=("selA",))
    S.add("pool", lambda e: e.memset(selA[:, 0:64], 1.0), writes=("selA",))
    S.add("pool", lambda e: e.memset(selB, 0.0), writes=("selB",))
    S.add("pool", lambda e: e.memset(selB[:, 64:128], 1.0), writes=("selB",))
    S.add("pool", lambda e: e.memset(zq, 0.0), writes=tuple(("zq", j) for j in range(4)))
    S.add("pool", lambda e: e.memset(Rst, 0.0), writes=("Rst",))
    for i_ in range(2):
        S.add("pool", lambda e, i_=i_: e.memset(Sbf[i_], 0.0), writes=(("Sbf", i_),))
    S.add("pool", lambda e: e.memset(dcol, 1.0), writes=("dcol",))
    S.add("pool", lambda e: e.memset(vaug, 0.0), writes=tuple(("vaug", c) for c in range(NCH)))
    S.add("pool", lambda e: e.memset(qz, 0.0), writes=tuple(("qz", h) for h in range(4)))

    for g in range(0 if not (skip & 4) else 4, 4):
        S.add("pe", lambda e, g=g: e.transpose(out=psb[4][:, g * 128:(g + 1) * 128],
                                               in_=wsg32[:, g, :], identity=identf),
              reads=(XB1, "wsg32", "identf"), writes=(PS(4),))
    S.add("dve", lambda e: e.tensor_copy(out=WmT32, in_=psb[4].rearrange("p (g s) -> p g s", g=4)),
          reads=(PS(4),), writes=(XB1,))
    S.add("dve", lambda e: e.memset(WmT32[64:128, :, 0:64], 0.0), reads=(XB1,), writes=(XB1,))
    S.add("dve", lambda e: e.tensor_copy(out=WmT, in_=WmT32), reads=(XB1,), writes=("WmT",))
    for g in range(4):
        S.add("pe", lambda e, g=g: e.matmul(psb[5][:, g * 128:(g + 1) * 128], lhsT=onesf,
                                            rhs=WmT32[:, g, :], start=True, stop=True),
              reads=(XB1, "onesf"), writes=(PS(5),))
    for g in range(4):
        S.add("dve", lambda e, g=g: e.scalar_tensor_tensor(
            out=T2[:, g, :], in0=psb[5][:, g * 128:(g + 1) * 128], scalar=lnbc[:, g:g + 1],
            in1=biasB[:, g, :], op0=ALU.mult, op1=ALU.add),
            reads=(PS(5), "lnbc", XB1b, "biasB"), writes=("T2",))
    S.add("dve", lambda e: e.tensor_copy(out=lngB, in_=lngc.unsqueeze(2).to_broadcast([128, 4, 128])),
          reads=("lngc",), writes=("lngB",))

    def conv_gu_blocks(n, fps):
        for fp in fps:
            b = BLK_GU[n][fp]
            conv_dma(wsc[b][:, 0:2048].rearrange("p (k m) -> p k m", k=8),
                     kview(wg_d[n])[:, :, fp * 256:(fp + 1) * 256], ("scr", b, 0), ("cvgu", n, fp if n == 0 else 0))
            conv_dma(wsc[b][:, 2048:4096].rearrange("p (k m) -> p k m", k=8),
                     kview(wu_d[n])[:, :, fp * 256:(fp + 1) * 256], ("scr", b, 1), ("cvgu", n, fp if n == 0 else 0))

    def conv_wd_parts(n, qs):
        wv = wd_d[n].rearrange("(f p) n -> p f n", p=128)
        for q in qs:
            f0, f1 = ((0, 6), (6, 11), (11, 17), (17, 22))[q]
            conv_dma(wdsc[n][:, f0:f1, :], wv[:, f0:f1, :], ("wdscr", n, q), ("cvd", n, q if n == 0 else 0))

    conv_gu_blocks(0, range(0, 3))
    conv_wd_parts(0, (0, 1))
    conv_gu_blocks(0, range(3, 8))
    conv_wd_parts(0, (2, 3))
    conv_gu_blocks(0, range(8, 11))
    conv_dma(wifsc, kview(win_d)[:, :, 2560:2568], ("wifscr",), ("cvmix",))
    conv_in(BLK_QK, 1024)
    conv_in(BLK_VM, 1536)
    conv_in(BLK_U, 0)
    conv_in(BLK_VS, 512)
    conv_in(BLK_O, 2048)
    for half, b in enumerate((BLK_WO0, BLK_WO1)):
        conv_dma(wsc[b].rearrange("p (k m) -> p k m", k=8),
                 wout_d.rearrange("(kc p) n -> p kc n", p=128)[:, :, half * 512:(half + 1) * 512],
                 ("scr", b, 0), ("cvmix", b))
    late_conv = []
    for fp in range(NFH):
        late_conv.append(lambda fp=fp: conv_gu_blocks(1, (fp,)))
        if fp in (2, 3):
            late_conv.append(lambda fp=fp: conv_wd_parts(1, (2 * (fp - 2), 2 * (fp - 2) + 1)))

    def drain_conv(k=1):
        for _ in range(k):
            if late_conv:
                late_conv.pop(0)()

    ring_ctr = [0]

    def ring_load(b):
        s = ring_ctr[0] % NRING
        ring_ctr[0] += 1
        S.add("sp", lambda e, s=s, b=b: e.dma_start(out=ring[s], in_=wsc[b]),
              reads=(("scr", b, 0), ("scr", b, 1)), writes=(("ring", s),), dma=("ring", s))
        return s

    plan = []
    plan += BLK_GU[0][0:6]
    for i in range(ntiles):
        plan += BLK_GU[0][5:11] + [BLK_QK, BLK_VM, BLK_U, BLK_VS, BLK_O]
        if i + 1 < ntiles:
            plan += BLK_GU[0][0:6]
        plan += [BLK_WO0, BLK_WO1] + BLK_GU[1][0:6] + BLK_GU[1][5:11]
    plan_pos = [0]
    issued = []

    def prefetch(depth):
        while len(issued) < depth and plan_pos[0] < len(plan):
            issued.append((plan[plan_pos[0]], ring_load(plan[plan_pos[0]])))
            plan_pos[0] += 1

    def next_block(expect):
        prefetch(1)
        b, s = issued.pop(0)
        assert b == expect, (b, expect)
        return s

    def load_wd(n, half):
        fl = (0, 6, 11) if half == 0 else (11, 17, 22)
        for q in range(2):
            f0, f1 = fl[q], fl[q + 1]
            S.add("sp", lambda e, n=n, f0=f0, f1=f1, half=half:
                  e.dma_start(out=wdb[:, f0 - 11 * half:f1 - 11 * half, :], in_=wdsc[n][:, f0:f1, :]),
                  reads=(("wdscr", n, 2 * half + q),), writes=(("wdb", q),), dma=("wdb", q))

    def load_x(i):
        buf = i % 2
        src = x_d[i * NT:(i + 1) * NT, :].rearrange("(c p) d -> p c d", p=128)
        S.add("sp", lambda e, buf=buf, src=src: e.dma_start(out=xb[buf], in_=src),
              writes=XALL(buf), dma=("xld", buf))

    def rsqrt_cols(src, dst, scale, rres, wres):
        S.add("dve", lambda e: e.tensor_scalar(out=dst, in0=src, scalar1=scale, scalar2=EPS,
                                               op0=ALU.mult, op1=ALU.add), reads=rres, writes=wres)
        S.add("act", lambda e: e.activation(out=dst, in_=dst, func=AF.Sqrt), reads=wres, writes=wres)
        S.add("dve", lambda e: e.reciprocal(out=dst, in_=dst), reads=wres, writes=wres)

    class Pipe:
        def __init__(self):
            self.items = []

        def at(self, step, fn):
            self.items.append((step, fn))

        def tick(self, step):
            for it in [it for it in self.items if it[0] <= step]:
                self.items.remove(it)
                it[1]()

        def flush(self):
            self.tick(10 ** 9)

    def stats_a(buf, c, col):
        xt = xb[buf]
        X = XR(buf, c)
        S.add("act", lambda e: e.activation(out=junk, in_=xt[:, c, :], func=AF.Square,
                                            accum_out=ssq[:, col:col + 1]),
              reads=(X,), writes=("junk", ("ssq", col)))
        S.add("act", lambda e: e.activation(out=rstd[:, col:col + 1], in_=ssq[:, col:col + 1], func=AF.Sqrt,
                                            scale=1.0 / D, bias=cst[:, 2:3]),
              reads=(("ssq", col), "cst"), writes=(("rstd", col),))

    def h_div(buf, c, col):
        xt = xb[buf]
        S.add("dve", lambda e: e.reciprocal(out=rstd[:, col:col + 1], in_=rstd[:, col:col + 1]),
              reads=(("rstd", col),), writes=(("rstd", col),))
        S.add("dve", lambda e: e.tensor_scalar(out=htok[c], in0=xt[:, c, :], scalar1=rstd[:, col:col + 1],
                                               scalar2=None, op0=ALU.mult),
              reads=(XR(buf, c), ("rstd", col)), writes=(("htok", c),))

    def h_T(c, gi, dst=None, dname="hT", bk=None):
        dst = hT if dst is None else dst
        hb = htok[c]
        HB = ("htok", c)
        bk = (6 + c % 2) if bk is None else bk
        pt = bankb[bk]
        PT = PS(bk)
        for k in range(8):
            S.add("pe", lambda e, k=k: e.transpose(out=pt[:, k * 128:(k + 1) * 128],
                                                   in_=hb[:, k * 128:(k + 1) * 128], identity=ident),
                  reads=(HB, "ident"), writes=(PT,))
        S.add("dve", lambda e: e.tensor_tensor(
            out=dst[:, :, c * 128:(c + 1) * 128], in0=pt.rearrange("p (k t) -> p k t", k=8),
            in1=gcol[:, gi, :].unsqueeze(2).to_broadcast([128, 8, 128]), op=ALU.mult),
            reads=(PT, ("gcol", gi)), writes=((dname, c),))

    def fin_scale(buf, c, col):
        xt = xb[buf]
        X = XR(buf, c)
        S.add("dve", lambda e: e.reciprocal(out=rstd[:, col:col + 1], in_=rstd[:, col:col + 1]),
              reads=(("rstd", col),), writes=(("rstd", col),))
        S.add("dve", lambda e: e.scalar_tensor_tensor(
            out=xt[:, c, :], in0=xt[:, c, :], scalar=rstd[:, col:col + 1], in1=gfin,
            op0=ALU.mult, op1=ALU.mult), reads=(X, ("rstd", col), "gfin"), writes=(X,))

    def norm_pipe(buf, gi, col0, lag=1, dst=None, dname="hT"):
        p = Pipe()
        for c in range(NCH):
            p.at(2 * c + lag, lambda c=c: stats_a(buf, c, col0 + c))
            p.at(2 * c + lag + 1, lambda c=c: h_div(buf, c, col0 + c))
            p.at(2 * c + lag + 2, lambda c=c: h_T(c, gi, dst, dname))
        return p

    def store_tile(i):
        buf = i % 2
        dst = out_d[i * NT:(i + 1) * NT, :].rearrange("(c p) d -> p c d", p=128)
        S.add("sp", lambda e: e.dma_start(out=dst, in_=xb[buf]),
              reads=XALL(buf), writes=(("outd", i),), dma=("xst", buf))

    def ffn_gu(i, n, half, src, srcres, overlap=False, hook=None):
        slot = None
        for fp in range(NFH):
            f0 = half * NFH + fp
            if f0 % 2 == 0 or fp == 0:
                slot = next_block(BLK_GU[n][f0 // 2])
                prefetch(3)
            if fp == 2:
                load_wd(n, half)
            if hook is not None and fp == 6:
                hook()
            f2 = f0 % 2
            RS = ("ring", slot)
            gi_ = f0 % 2
            ui_ = 2 if overlap else 2 + f0 % 2
            pg, pu = psb[gi_], psb[ui_]
            for k in range(8):
                S.add("pe", lambda e, k=k, slot=slot, f2=f2, pg=pg: e.matmul(
                    pg, lhsT=ring[slot][:, k * 256 + f2 * 128:k * 256 + f2 * 128 + 128],
                    rhs=src[:, k, :], start=(k == 0), stop=(k == 7)),
                    reads=(RS,) + srcres, writes=(PS(gi_),))
                if overlap and k == 3:
                    yield
            if overlap:
                yield
            for k in range(8):
                S.add("pe", lambda e, k=k, slot=slot, f2=f2, pu=pu: e.matmul(
                    pu, lhsT=ring[slot][:, 2048 + k * 256 + f2 * 128:2048 + k * 256 + f2 * 128 + 128],
                    rhs=src[:, k, :], start=(k == 0), stop=(k == 7)),
                    reads=(RS,) + srcres, writes=(PS(ui_),))
                if overlap and k == 3:
                    yield
            sg = sgs[f0 % 2]
            S.add("act", lambda e, pg=pg, sg=sg: e.activation(out=sg, in_=pg, func=AF.Silu),
                  reads=(PS(gi_),), writes=(("sgs", f0 % 2),))
            S.add("dve", lambda e, pu=pu, sg=sg, fp=fp: e.tensor_tensor(
                out=actb[:, fp, :], in0=sg, in1=pu, op=ALU.mult),
                reads=(PS(ui_), ("sgs", f0 % 2)), writes=(("actb", fp),))
            yield

    def ffn_down(i, n, half, pipe=None):
        buf = i % 2
        xt = xb[buf]
        for c in range(NCH):
            for dh in range(2):
                pi = 4 + (2 * c + dh) % 2
                po = psb[pi]
                for fp in range(NFH):
                    S.add("pe", lambda e, fp=fp, c=c, dh=dh, po=po: e.matmul(
                        po, lhsT=actb[:, fp, c * 128:(c + 1) * 128],
                        rhs=wdb[:, fp, dh * 512:(dh + 1) * 512],
                        start=(fp == 0), stop=(fp == NFH - 1)),
                        reads=(("actb", fp), ("wdb", 0 if fp < 6 else 1)), writes=(PS(pi),))
                S.add("dve", lambda e, c=c, dh=dh, po=po: e.scalar_tensor_tensor(
                    out=xt[:, c, dh * 512:(dh + 1) * 512], in0=po, scalar=0.5,
                    in1=xt[:, c, dh * 512:(dh + 1) * 512], op0=ALU.mult, op1=ALU.add),
                    reads=(PS(pi), XR(buf, c)), writes=(XR(buf, c),))
                if pipe is not None:
                    pipe.tick(2 * c + dh)
        if pipe is not None:
            pipe.flush()

    def run(gen):
        for _ in gen:
            pass

    def mixer(i, pipe=None, pre_a=None, pre_e=None, inter=None, pre_g=None):
        buf = i % 2
        xt = xb[buf]

        def pull(k):
            if inter is not None:
                for _ in range(k):
                    next(inter, None)

        def gates():
            for c in range(NCH):
                if c == 3 and pre_g is not None:
                    pre_g()
                for k in range(8):
                    S.add("pe", lambda e, c=c, k=k: e.matmul(
                        ps7[:, c * 8:(c + 1) * 8], lhsT=hT[:, k, c * 128:(c + 1) * 128], rhs=wif[:, k, :],
                        start=(k == 0), stop=(k == 7)), reads=(("hT", c), "wif"), writes=(PS(7),))
            S.add("dve", lambda e: e.tensor_tensor(
                out=gts, in0=ps7[:, 0:32].rearrange("p (c g) -> p c g", c=NCH),
                in1=gbias.unsqueeze(1).to_broadcast([128, NCH, 8]), op=ALU.add),
                reads=(PS(7), "gbi", "gbf"), writes=("gts",))
            S.add("act", lambda e: e.activation(out=eft, in_=gts[:, :, 4:8], func=AF.Exp, scale=-1.0),
                  reads=("gts",), writes=("eft",))
            S.add("act", lambda e: e.activation(out=lt, in_=eft, func=AF.Ln, bias=cst[:, 0:1]),
                  reads=("eft", "cst"), writes=("lt",))

        def gates_b():
            for c in range(NCH):
                S.add("pe", lambda e, c=c: e.matmul(ps7[:, 32 + c * 4:36 + c * 4], lhsT=tri, rhs=lt[:, c, :],
                                                    start=True, stop=True),
                      reads=("tri", "lt"), writes=(PS(7),))
                S.add("pe", lambda e, c=c: e.matmul(ps7[:, 48 + c * 2:50 + c * 2], lhsT=selA,
                                                    rhs=lt[:, c, 0:4:2], start=True, stop=False),
                      reads=("selA", "lt"), writes=(PS(7),))
                S.add("pe", lambda e, c=c: e.matmul(ps7[:, 48 + c * 2:50 + c * 2], lhsT=selB,
                                                    rhs=lt[:, c, 1:4:2], start=False, stop=True),
                      reads=("selB", "lt"), writes=(PS(7),))
            bpv = ps7[:, 32:48].rearrange("p (c h) -> p c h", c=NCH)
            S.add("dve", lambda e: e.tensor_tensor(out=t1t, in0=bpv, in1=gts[:, :, 0:4], op=ALU.add),
                  reads=(PS(7), "gts"), writes=("t1t",))
            S.add("act", lambda e: e.activation(out=ebt, in_=bpv, func=AF.Exp),
                  reads=(PS(7),), writes=("ebt",))
            S.add("act", lambda e: e.activation(out=dcn, in_=ps7[:, 48:56].rearrange("p (c j) -> p c j", c=NCH),
                                                func=AF.Exp, scale=-1.0),
                  reads=(PS(7),), writes=("dcn",))
            S.add("act", lambda e: e.activation(out=kap, in_=t1t, func=AF.Exp, bias=cst[:, 1:2]),
                  reads=("t1t", "cst"), writes=("kap",))

        def qk_proj(j, slot):
            RS = ("ring", slot)
            pz = psb[j % 2]
            for k in range(8):
                S.add("pe", lambda e, k=k: e.matmul(
                    pz, lhsT=ring[slot][:, k * 512 + j * 128:k * 512 + (j + 1) * 128], rhs=hT[:, k, :],
                    start=(k == 0), stop=(k == 7)), reads=(RS,) + HTALL, writes=(PS(j % 2),))
            ZQ = ("zq", j)
            S.add("act", lambda e: e.activation(out=zq[:, j, 3:NT + 3], in_=pz, func=AF.Copy),
                  reads=(PS(j % 2),), writes=(ZQ,))
            ca = cacc[j % 2]
            CA = ("cacc", j % 2)
            S.add("pool", lambda e: e.tensor_scalar(
                out=ca, in0=zq[:, j, 0:NT], scalar1=cw[:, j, 0:1], scalar2=cb[:, j:j + 1],
                op0=ALU.mult, op1=ALU.add), reads=(ZQ, ("cw", 0), "cb"), writes=(CA,))
            for tp in range(1, 4):
                S.add("dve", lambda e, tp=tp: e.scalar_tensor_tensor(
                    out=ca, in0=zq[:, j, tp:NT + tp], scalar=cw[:, j, tp:tp + 1], in1=ca,
                    op0=ALU.mult, op1=ALU.add), reads=(ZQ, ("cw", tp), CA), writes=(CA,))
            S.add("pool", lambda e: e.tensor_copy(out=zq[:, j, 0:3], in_=zq[:, j, NT:NT + 3]),
                  reads=(ZQ,), writes=(ZQ,))

        def qk_silu(j):
            ca = cacc[j % 2]
            CA = ("cacc", j % 2)
            if j < 2:
                S.add("act", lambda e: e.activation(out=qz[0:64, 2 * j, :], in_=ca[0:64, :], func=AF.Silu),
                      reads=(CA,), writes=(("qz", 2 * j),))
                S.add("act", lambda e: e.activation(out=qz[64:128, 2 * j + 1, :], in_=ca[64:128, :],
                                                    func=AF.Silu), reads=(CA,), writes=(("qz", 2 * j + 1),))
            else:
                S.add("act", lambda e: e.activation(out=qkT[:, j, :], in_=ca, func=AF.Silu),
                      reads=(CA,), writes=(("qkT", j),))

        def k_tr(c):
            for jj in range(2):
                S.add("pe", lambda e, jj=jj: e.transpose(
                    out=psT[:, jj * 128:(jj + 1) * 128], in_=qkT[:, 2 + jj, c * 128:(c + 1) * 128],
                    identity=ident), reads=(("qkT", 2 + jj), "ident"), writes=(PS(6),))
            S.add("act", lambda e: e.activation(out=ktok[:, c, :], in_=psT[:, 0:256], func=AF.Copy),
                  reads=(PS(6),), writes=(("ktok", c),))

        def vm_proj(c, slot):
            RS = ("ring", slot)
            pz = psb[2 + c]
            for k in range(8):
                S.add("pe", lambda e, k=k: e.matmul(
                    pz, lhsT=hT[:, k, c * 128:(c + 1) * 128], rhs=ring[slot][:, k * 512:(k + 1) * 512],
                    start=(k == 0), stop=(k == 7)), reads=(RS, ("hT", c)), writes=(PS(2 + c),))
            S.add("dve", lambda e: e.tensor_tensor(
                out=vaug[:, c, :, 0:128], in0=pz.rearrange("p (h v) -> p h v", h=4),
                in1=kap[:, c, :].unsqueeze(2).to_broadcast([128, 4, 128]), op=ALU.mult),
                reads=(PS(2 + c), "kap"), writes=(("vaug", c),))
            S.add("pool", lambda e: e.tensor_copy(out=vaug[:, c, :, 128:129], in_=kap[:, c, :].unsqueeze(2)),
                  reads=("kap",), writes=(("vaug", c),))

        def u_proj(g, slot):
            RS = ("ring", slot)
            pz = psb[g % 2]
            for k in range(8):
                S.add("pe", lambda e, k=k: e.matmul(
                    pz, lhsT=ring[slot][:, k * 512 + g * 128:k * 512 + (g + 1) * 128], rhs=hT[:, k, :],
                    start=(k == 0), stop=(k == 7)), reads=(RS,) + HTALL, writes=(PS(g % 2),))
            S.add("act", lambda e: e.activation(out=ug[:, g, :], in_=pz, func=AF.Gelu),
                  reads=(PS(g % 2),), writes=(("ug", g),))

        def vs_proj(c, slot):
            RS = ("ring", slot)
            pz = psb[2 + c % 2]
            for k in range(8):
                S.add("pe", lambda e, k=k: e.matmul(
                    pz, lhsT=hT[:, k, c * 128:(c + 1) * 128], rhs=ring[slot][:, k * 512:(k + 1) * 512],
                    start=(k == 0), stop=(k == 7)), reads=(RS, ("hT", c)), writes=(PS(2 + c % 2),))
            vg = vgs[c]
            VG = ("vgs", c)
            S.add("act", lambda e: e.activation(out=vg, in_=pz, func=AF.Gelu, accum_out=st1[:, c:c + 1]),
                  reads=(PS(2 + c % 2),), writes=(VG, ("st1", c)))
            S.add("act", lambda e: e.activation(out=junk[:, 0:NT], in_=vg, func=AF.Square,
                                                accum_out=st2[:, c:c + 1]),
                  reads=(VG,), writes=("junk", ("st2", c)))

        def ln_stats():
            ST = ("sgst",)
            st1r = tuple(("st1", c) for c in range(NCH))
            st2r = tuple(("st2", c) for c in range(NCH))
            S.add("dve", lambda e: e.tensor_scalar(out=mu_t, in0=st1[:, 0:4], scalar1=1.0 / 512, scalar2=None,
                                                   op0=ALU.mult), reads=st1r, writes=(ST,))
            S.add("dve", lambda e: e.tensor_tensor(out=var_t, in0=mu_t, in1=mu_t, op=ALU.mult),
                  reads=(ST,), writes=(ST,))
            S.add("dve", lambda e: e.scalar_tensor_tensor(out=var_t, in0=st2[:, 0:4], scalar=1.0 / 512, in1=var_t,
                                                          op0=ALU.mult, op1=ALU.subtract),
                  reads=(ST,) + st2r, writes=(ST,))
            S.add("act", lambda e: e.activation(out=rs_t, in_=var_t, func=AF.Sqrt, bias=cst[:, 2:3]),
                  reads=(ST, "cst"), writes=(ST,))
            S.add("dve", lambda e: e.reciprocal(out=rs_t, in_=rs_t), reads=(ST,), writes=(ST,))
            S.add("dve", lambda e: e.scalar_tensor_tensor(out=nmr_t, in0=mu_t, scalar=-1.0, in1=rs_t,
                                                          op0=ALU.mult, op1=ALU.mult), reads=(ST,), writes=(ST,))

        def sgu_chunk(c):
            ST = ("sgst",)
            vg = vgs[c]
            VG = ("vgs", c)
            vh = vhat[c % 2]
            VH = ("vhat", c % 2)
            S.add("pool", lambda e: e.tensor_scalar(
                out=vh, in0=vg, scalar1=rs_t[:, c:c + 1], scalar2=nmr_t[:, c:c + 1],
                op0=ALU.mult, op1=ALU.add), reads=(VG, ST), writes=(VH,))
            pull(1)
            for g in range(4):
                S.add("pe", lambda e, g=g: e.matmul(
                    bank[6][:, g * 128:(g + 1) * 128], lhsT=vh[:, g * 128:(g + 1) * 128], rhs=WmT[:, g, :],
                    start=True, stop=True), reads=(VH, "WmT"), writes=(PS(6),))
            S.add("dve", lambda e: e.tensor_tensor(
                out=tmpA, in0=bank[6].rearrange("p (g t) -> p g t", g=4), in1=lngB, op=ALU.mult),
                reads=(PS(6), "lngB"), writes=("tmpA",))
            S.add("pool", lambda e: e.tensor_tensor(out=tmpB, in0=tmpA, in1=T2, op=ALU.add),
                  reads=("tmpA", "T2"), writes=("tmpB",))
            S.add("pool", lambda e: e.tensor_tensor(
                out=yT[:, 0:4, c * 128:(c + 1) * 128], in0=tmpB, in1=ug[:, :, c * 128:(c + 1) * 128],
                op=ALU.mult), reads=("tmpB",) + tuple(("ug", g) for g in range(4)), writes=(("yTa", c),))

        def o_proj(jo, slot):
            RS = ("ring", slot)
            pz = psb[jo % 2]
            for k in range(8):
                S.add("pe", lambda e, k=k: e.matmul(
                    pz, lhsT=ring[slot][:, k * 512 + jo * 128:k * 512 + (jo + 1) * 128], rhs=hT[:, k, :],
                    start=(k == 0), stop=(k == 7)), reads=(RS,) + HTALL, writes=(PS(jo % 2),))
            S.add("act", lambda e: e.activation(out=sigo[:, jo, :], in_=pz, func=AF.Sigmoid),
                  reads=(PS(jo % 2),), writes=(("sigo", jo),))

        QZ = tuple(("qz", h) for h in range(4))

        def e_a(c):
            tcs = slice(c * 128, (c + 1) * 128)
            pull(1)
            for h in range(4):
                S.add("pe", lambda e, h=h: e.matmul(
                    psb[4][:, h * 128:(h + 1) * 128], lhsT=qkT[:, 2 + h // 2, tcs],
                    rhs=qz[:, h, tcs], start=True, stop=True),
                    reads=(("qkT", 2 + h // 2), ("qz", h)), writes=(PS(4),))
            wt = wTt[c % 2]
            WT = ("wTt", c % 2)
            S.add("dve", lambda e: e.tensor_tensor(
                out=wt, in0=psb[4].rearrange("p (h t) -> p h t", h=4),
                in1=tri.unsqueeze(1).to_broadcast([128, 4, 128]), op=ALU.mult),
                reads=(PS(4), "tri"), writes=(WT,))
            SB = ("Sbf", c % 2)
            S.add("pool", lambda e: e.tensor_tensor(
                out=Sbf[c % 2][:, :, 0:129], in0=Rst[:, :, 0:129],
                in1=dcol.unsqueeze(2).to_broadcast([128, 2, 129]), op=ALU.mult),
                reads=("Rst", "dcol"), writes=(SB,))
            for j in range(2):
                pull(1)
                for a in range(2):
                    h = 2 * j + a
                    S.add("pe", lambda e, h=h, j=j, a=a: e.matmul(
                        bank[5 + 2 * j][:, a * 130:a * 130 + 129], lhsT=ktok[:, c, j * 128:(j + 1) * 128],
                        rhs=vaug[:, c, h, 0:129], start=True, stop=True),
                        reads=(("ktok", c), ("vaug", c)), writes=(PS(5 + 2 * j),))
                for a in range(2):
                    p0 = a * 64
                    S.add("dve", lambda e, j=j, a=a, p0=p0: e.scalar_tensor_tensor(
                        out=Rst[p0:p0 + 64, j, 0:129], in0=Rst[p0:p0 + 64, j, 0:129],
                        scalar=dcol[p0:p0 + 64, j:j + 1],
                        in1=bank[5 + 2 * j][p0:p0 + 64, a * 130:a * 130 + 129],
                        op0=ALU.mult, op1=ALU.add),
                        reads=("Rst", "dcol", PS(5 + 2 * j)), writes=("Rst",))
            S.add("pool", lambda e: e.tensor_copy(out=dcol, in_=dcn[:, c, :]),
                  reads=("dcn",), writes=("dcol",))

        def e_b(c):
            tcs = slice(c * 128, (c + 1) * 128)
            wt = wTt[c % 2]
            WT = ("wTt", c % 2)
            SB = ("Sbf", c % 2)
            for j in range(2):
                pull(1)
                for a in range(2):
                    h = 2 * j + a
                    po = psb[3 + j][:, a * 130:a * 130 + 129]
                    S.add("pe", lambda e, h=h, po=po: e.matmul(
                        po, lhsT=wt[:, h, :], rhs=vaug[:, c, h, 0:129], start=True, stop=False),
                        reads=(WT, ("vaug", c)), writes=(PS(3 + j),))
                    S.add("pe", lambda e, h=h, j=j, po=po: e.matmul(
                        po, lhsT=qz[:, h, tcs], rhs=Sbf[c % 2][:, j, 0:129], start=False, stop=True),
                        reads=(("qz", h), SB), writes=(PS(3 + j),))
                S.add("act", lambda e, j=j: e.activation(out=numsb[:, j, :], in_=psb[3 + j][:, 0:260], func=AF.Copy),
                      reads=(PS(3 + j),), writes=(("numsb", j),))
            for j in range(2):
                pn = numsb[:, j, :].rearrange("p (a v) -> p a v", a=2)
                S.add("dve", lambda e, j=j, pn=pn: e.tensor_tensor(
                    out=dabs[:, 2 * j:2 * j + 2].unsqueeze(2), in0=pn[:, :, 128:129],
                    in1=ebt[:, c, 2 * j:2 * j + 2].unsqueeze(2), op=ALU.max),
                    reads=(("numsb", j), "ebt"), writes=(("dabs", j),))
                S.add("dve", lambda e, j=j, pn=pn: e.scalar_tensor_tensor(
                    out=dabs[:, 2 * j:2 * j + 2].unsqueeze(2), in0=pn[:, :, 128:129], scalar=-1.0,
                    in1=dabs[:, 2 * j:2 * j + 2].unsqueeze(2), op0=ALU.mult, op1=ALU.max),
                    reads=(("numsb", j), ("dabs", j)), writes=(("dabs", j),))
                for a in range(2):
                    h = 2 * j + a
                    S.add("act", lambda e, h=h, a=a, pn=pn: e.activation(
                        out=junk[:, 0:128], in_=pn[:, a, 0:128], func=AF.Square,
                        accum_out=ssqn[:, h:h + 1]), reads=(("numsb", j),), writes=("junk", ("ssqn", h)))
            HS = ("hstat",)
            S.add("dve", lambda e: e.reciprocal(out=rcp, in_=dabs),
                  reads=(("dabs", 0), ("dabs", 1)), writes=(HS,))
            S.add("dve", lambda e: e.tensor_tensor(out=sct, in0=rcp, in1=rcp, op=ALU.mult),
                  reads=(HS,), writes=(HS,))
            S.add("dve", lambda e: e.tensor_tensor(out=sct, in0=sct, in1=ssqn, op=ALU.mult),
                  reads=(HS,) + tuple(("ssqn", h) for h in range(4)), writes=(HS,))
            S.add("act", lambda e: e.activation(out=sct, in_=sct, func=AF.Sqrt, scale=1.0 / 128, bias=cst[:, 2:3]),
                  reads=(HS, "cst"), writes=(HS,))
            S.add("dve", lambda e: e.reciprocal(out=sct, in_=sct), reads=(HS,), writes=(HS,))
            S.add("dve", lambda e: e.tensor_tensor(out=sc2, in0=sct, in1=rcp, op=ALU.mult),
                  reads=(HS,), writes=(("sc2",),))
            hb = hn[c % 2]
            HN = ("hn", c % 2)
            for j in range(2):
                pn = numsb[:, j, :].rearrange("p (a v) -> p a v", a=2)
                S.add("dve", lambda e, j=j, pn=pn: e.tensor_tensor(
                    out=hb[:, 2 * j:2 * j + 2, :], in0=pn[:, :, 0:128],
                    in1=sc2[:, 2 * j:2 * j + 2].unsqueeze(2).to_broadcast([128, 2, 128]), op=ALU.mult),
                    reads=(("numsb", j), ("sc2",)), writes=(HN,))

        def e_c(c):
            tcs = slice(c * 128, (c + 1) * 128)
            hb = hn[c % 2]
            HN = ("hn", c % 2)
            pull(1)
            for h in range(4):
                S.add("pe", lambda e, h=h: e.transpose(
                    out=psT[:, h * 128:(h + 1) * 128], in_=hb[:, h, :], identity=ident),
                    reads=(HN, "ident"), writes=(PS(6),))
            for h in range(4):
                S.add("dve", lambda e, h=h: e.scalar_tensor_tensor(
                    out=yT[:, 4 + h, tcs], in0=psT[:, h * 128:(h + 1) * 128], scalar=mhnc[:, h:h + 1],
                    in1=sigo[:, h, tcs], op0=ALU.mult, op1=ALU.mult),
                    reads=(PS(6), "mhnc", ("sigo", h)), writes=(("yTb", c, h),))

        if pre_a is not None:
            pre_a()
        gates()
        slot = next_block(BLK_QK)
        prefetch(3)
        for j in range(4):
            qk_proj(j, slot)
            if j == 1:
                gates_b()
            if j >= 1:
                qk_silu(j - 1)
            drain_conv(1)
        qk_silu(3)
        slot = next_block(BLK_VM)
        prefetch(3)
        for c in range(NCH):
            vm_proj(c, slot)
            drain_conv(1)
        slot = next_block(BLK_U)
        prefetch(3)
        for g in range(4):
            u_proj(g, slot)
            drain_conv(1)
        slot = next_block(BLK_VS)
        prefetch(3)
        for c in range(NCH):
            vs_proj(c, slot)
            drain_conv(1)
        ln_stats()
        for c in range(NCH):
            k_tr(c)
        slot = next_block(BLK_O)
        prefetch(3)
        for jo in range(4):
            o_proj(jo, slot)
            drain_conv(1)
        if pre_e is not None:
            pre_e()

        for s_ in range(NCH + 2):
            if s_ < NCH:
                e_a(s_)
            pull(1)
            if 0 <= s_ - 1 < NCH:
                e_b(s_ - 1)
            if s_ < NCH:
                sgu_chunk(s_)
            pull(1)
            if 0 <= s_ - 2 < NCH:
                e_c(s_ - 2)
            drain_conv(1)
        pull(99)
        drain_conv(99)

        slots = (next_block(BLK_WO0), next_block(BLK_WO1))
        prefetch(2)
        for c in range(NCH):
            for dh in range(2):
                slot = slots[dh]
                RS = ("ring", slot)
                pi = (2 * c + dh) % 2
                po = psb[pi]
                for m in range(8):
                    S.add("pe", lambda e, m=m, c=c, slot=slot, po=po: e.matmul(
                        po, lhsT=yT[:, m, c * 128:(c + 1) * 128], rhs=ring[slot][:, m * 512:(m + 1) * 512],
                        start=(m == 0), stop=(m == 7)),
                        reads=(RS, ("yTa", c)) + tuple(("yTb", c, hh) for hh in range(4)), writes=(PS(pi),))
                S.add("dve", lambda e, c=c, dh=dh, po=po: e.tensor_tensor(
                    out=xt[:, c, dh * 512:(dh + 1) * 512], in0=po, in1=xt[:, c, dh * 512:(dh + 1) * 512],
                    op=ALU.add), reads=(PS(pi), XR(buf, c)), writes=(XR(buf, c),))
                if pipe is not None:
                    pipe.tick(2 * c + dh)

    norm_pipe(0, 0, 0, dst=hTf, dname="hTf").flush()
    prefetch(2)
    run(ffn_gu(0, 0, 0, hTf, HTFALL))
    S.add("sp", lambda e: e.dma_start(out=wif, in_=wifsc), reads=(("wifscr",),), writes=("wif",),
          dma=("cwif",))
    ffn_down(0, 0, 0)
    for i in range(ntiles):
        buf = i % 2
        more = i + 1 < ntiles
        nxt = (lambda i=i: load_x(i + 1)) if more else None
        run(ffn_gu(i, 0, 1, hTf, HTFALL, hook=nxt))
        pn = norm_pipe(buf, 1, 4)
        pn.items.pop(11)
        late = [lambda: h_T(3, 1, None, "hT", bk=6)]
        if more:
            for c in range(NCH):
                pn.at(2 * c + 1, lambda c=c, buf=buf: stats_a(1 - buf, c, c))
                if c < 3:
                    pn.at(2 * c + 4, lambda c=c, buf=buf: h_div(1 - buf, c, c))
            late.append(lambda buf=buf: h_div(1 - buf, 3, 3))
        ffn_down(i, 0, 1, pipe=pn)

        def pre_g(late=late):
            for fn_ in late:
                fn_()
        inter = ffn_gu(i + 1, 0, 0, hTf, HTFALL, overlap=True) if more else None
        def pre_a(buf=buf):
            for c in range(NCH):
                stats_a(1 - buf, c, c)
            for c in range(NCH):
                h_div(1 - buf, c, c)

        def pre_b():
            for c in range(NCH):
                h_T(c, 0, hTf, "hTf")
        pm = norm_pipe(buf, 2, 8)
        mixer(i, pipe=pm, pre_a=None, pre_e=pre_b if more else None, inter=inter, pre_g=pre_g)
        if more:
            rest = Pipe()
            for n_, it in enumerate(sorted(pm.items, key=lambda it: it[0])):
                rest.at(n_, it[1])
            ffn_down(i + 1, 0, 0, pipe=rest)
        else:
            pm.flush()
        p2 = Pipe()
        for c in range(NCH):
            p2.at(2 * c + 1, lambda c=c, buf=buf: stats_a(buf, c, 12 + c))
            p2.at(2 * c + 2, lambda c=c, buf=buf: fin_scale(buf, c, 12 + c))
        run(ffn_gu(i, 1, 0, hT, HTALL))
        ffn_down(i, 1, 0)
        run(ffn_gu(i, 1, 1, hT, HTALL))
        ffn_down(i, 1, 1, pipe=p2)
        store_tile(i)
    finals = [("xst", 0)] + ([("xst", 1)] if ntiles > 1 else [])
    if debug:
        dumps = {
            "yT": (yT, [("yTa", c) for c in range(4)] + [("yTb", c, h) for c in range(4) for h in range(4)]),
            "qz": (qz, [("qz", h) for h in range(4)]), "qkT": (qkT, [("qkT", 2), ("qkT", 3)]), "ug": (ug, [("ug", g) for g in range(4)]),
            "sigo": (sigo, [("sigo", g) for g in range(4)]), "vaug": (vaug, [("vaug", c) for c in range(4)]),
            "kap": (kap, ["kap"]), "ebt": (ebt, ["ebt"]), "dcn": (dcn, ["dcn"]), "gts": (gts, ["gts"]),
            "lt": (lt, ["lt"]), "ktok": (ktok, [("ktok", c) for c in range(4)]), "T2": (T2, ["T2"]),
            "WmT": (WmT, ["WmT"]), "Rst": (Rst, ["Rst"]), "hn1": (hn[1], [("hn", 1)]), "wT1": (wTt[1], [("wTt", 1)]),
            "zq": (zq, [("zq", j) for j in range(4)]), "vhat1": (vhat[1], [("vhat", 1)]),
            "tri": (tri, ["tri"]), "cw": (cw, [("cw", k) for k in range(4)]), "gcol": (gcol, [("gcol", k) for k in range(3)]),
            "hT": (hT, list(HTALL)), "st1": (st1, [("sgst",)]), "st2": (st2, [("sgst",)]), "mu_t": (mu_t, [("sgst",)]),
            "var_t": (var_t, [("sgst",)]), "rs_t": (rs_t, [("sgst",)]), "nmr_t": (nmr_t, [("sgst",)]), "eft": (eft, ["eft"]),
            "vgs3": (vgs[3], [("vgs", 3)]), "sc2": (sc2, [("sc2",)]), "rcp": (rcp, [("hstat",)]), "Sbf": (Sbf[0], [("Sbf", 0)]),
        }
        for name, (ap, rres) in dumps.items():
            dd = nc.dram_tensor("dbg_" + name, list(ap.shape), ap.dtype, kind="ExternalOutput").ap()
            S.add("sp", lambda e, dd=dd, ap=ap: e.dma_start(out=dd, in_=ap), reads=tuple(rres),
                  writes=(("dbgd", name),), dma=("dbg",), total=True)
        finals.append(("dbg",))
    S.emit(nc, finals)
    return nc


_DBG = dict(stop=9, skip=0, mstop=99, estop=99, debug=False, sink=None, pad=0)

_W_NAMES = ("ffn1_norm", "ffn1_w_gate", "ffn1_w_up", "ffn1_w_down", "mix_norm", "w_in", "sgu_ln_g",
            "sgu_ln_b", "sgu_w", "sgu_b", "conv_w", "conv_b", "igate_b", "fgate_b", "mh_norm", "w_out",
            "ffn2_norm", "ffn2_w_gate", "ffn2_w_up", "ffn2_w_down")


def kernel(**inputs):
    x = np.ascontiguousarray(np.asarray(inputs["x"], dtype=np.float32))
    B, T, _ = x.shape
    shared = {}
    for n in _W_NAMES:
        a = np.asarray(inputs[n], dtype=np.float32)
        shared[n] = np.ascontiguousarray(a[0])
    shared["final_norm"] = np.ascontiguousarray(np.asarray(inputs["final_norm"], dtype=np.float32))
    nc = build_program(T, _DBG['stop'], _DBG['skip'], _DBG['mstop'], _DBG['estop'], _DBG['debug'])
    in_maps = [dict(shared, x=x[b]) for b in range(B)]
    res = run_bass_kernel_spmd(nc, in_maps, core_ids=list(range(B)))
    if _DBG['sink'] is not None:
        _DBG['sink'](res.results)
    return np.stack([np.asarray(r["out"], dtype=np.float32) for r in res.results], axis=0)
```
